# Optimizing a Trainium2 kernel written in Bass

```python
import math
import jax, jax.numpy as jnp
from jax import lax
import numpy as np

D_MODEL = 1024
BATCH = 4
SEQ = 4096
DEPTH = 1

MEM_LEN = 256
HEAD_DIM = 64
N_Q_HEADS = 8
N_KV_HEADS = 2
GROUP = N_Q_HEADS // N_KV_HEADS
WINDOW = 128
BLOCK = 128
ATTN_W = N_Q_HEADS * HEAD_DIM
KV_W = N_KV_HEADS * HEAD_DIM
CONV_W = D_MODEL // 2
CONV_K = 31
N_CROSS_HEADS = 4
CROSS_HEAD_DIM = 128
CROSS_W = N_CROSS_HEADS * CROSS_HEAD_DIM
N_BRANCH = 3
RMS_EPS = 1e-6
LN_EPS = 1e-5
NEG_BIG = -1e30

SPLIT_SIZES = [ATTN_W, KV_W, KV_W, ATTN_W, 2 * CONV_W, CONV_W, CROSS_W, CROSS_W, N_BRANCH * D_MODEL]
IN_W = sum(SPLIT_SIZES)
SPLIT_POINTS = [int(v) for v in np.cumsum(SPLIT_SIZES)[:-1]]

ALIBI_SLOPES = np.array([2.0 ** (-8.0 * (i + 1) / N_Q_HEADS) for i in range(N_Q_HEADS)], dtype=np.float32)

kernel_name = "hybrid_swa_conformer_conv_memxattn_gated"


def rmsnorm(x, g):
    xf = x.astype(jnp.float32)
    y = xf * lax.rsqrt(jnp.mean(xf * xf, axis=-1, keepdims=True) + RMS_EPS)
    return (y * g.astype(jnp.float32)).astype(x.dtype)


def layernorm(x, g, b):
    xf = x.astype(jnp.float32)
    mu = jnp.mean(xf, axis=-1, keepdims=True)
    var = jnp.mean(jnp.square(xf - mu), axis=-1, keepdims=True)
    y = (xf - mu) * lax.rsqrt(var + LN_EPS)
    return (y * g.astype(jnp.float32) + b.astype(jnp.float32)).astype(x.dtype)


def _band(t, nb):
    B = t.shape[0]
    pad = jnp.zeros((B, BLOCK) + t.shape[2:], t.dtype)
    tp = jnp.concatenate([pad, t], axis=1).reshape((B, nb + 1, BLOCK) + t.shape[2:])
    return jnp.concatenate([tp[:, :-1], tp[:, 1:]], axis=2)


def sliding_window_attention(q, k, v, pos, sinks):
    B, S = q.shape[0], q.shape[1]
    nb = S // BLOCK
    qb = q.reshape(B, nb, BLOCK, N_KV_HEADS, GROUP, HEAD_DIM)
    kw = _band(k, nb)
    vw = _band(v, nb)
    pad_pos = jnp.broadcast_to(pos[:, :1] - (WINDOW + 1), (B, BLOCK))
    pos_p = jnp.concatenate([pad_pos, pos], axis=1).reshape(B, nb + 1, BLOCK)
    kpos = jnp.concatenate([pos_p[:, :-1], pos_p[:, 1:]], axis=2)
    qpos = pos.reshape(B, nb, BLOCK)
    delta = qpos[:, :, :, None] - kpos[:, :, None, :]
    allowed = (delta >= 0) & (delta < WINDOW)
    scale = 1.0 / math.sqrt(HEAD_DIM)
    s = jnp.einsum('bnqhgd,bnkhd->bnhgqk', qb, kw).astype(jnp.float32) * scale
    slopes = jnp.asarray(ALIBI_SLOPES).reshape(N_KV_HEADS, GROUP)
    s = s - slopes[None, None, :, :, None, None] * delta[:, :, None, None, :, :].astype(jnp.float32)
    s = jnp.where(allowed[:, :, None, None, :, :], s, NEG_BIG)
    sink = jnp.broadcast_to(sinks.astype(jnp.float32).reshape(N_KV_HEADS, GROUP)[None, None, :, :, None, None],
                            s.shape[:-1] + (1,))
    p = jax.nn.softmax(jnp.concatenate([s, sink], axis=-1), axis=-1)[..., :-1]
    o = jnp.einsum('bnhgqk,bnkhd->bnqhgd', p.astype(v.dtype), vw)
    return o.reshape(B, S, ATTN_W)


def conformer_conv(u, b_glu, w_dw, b_dw, ln_g, ln_b):
    u = u + b_glu
    a = u[..., :CONV_W] * jax.nn.sigmoid(u[..., CONV_W:])
    c = lax.conv_general_dilated(a, w_dw, window_strides=(1,), padding=[(CONV_K - 1, 0)],
                                 dimension_numbers=('NWC', 'WIO', 'NWC'),
                                 feature_group_count=CONV_W) + b_dw
    return jax.nn.silu(layernorm(c, ln_g, ln_b))


def memory_cross_attention(qc, mem_n, w_mem_kv):
    B, S = qc.shape[0], qc.shape[1]
    kv = mem_n @ w_mem_kv
    km = kv[..., :CROSS_W].reshape(B, MEM_LEN, N_CROSS_HEADS, CROSS_HEAD_DIM)
    vm = kv[..., CROSS_W:].reshape(B, MEM_LEN, N_CROSS_HEADS, CROSS_HEAD_DIM)
    q = qc.reshape(B, S, N_CROSS_HEADS, CROSS_HEAD_DIM)
    s = jnp.einsum('bshd,bmhd->bhsm', q, km).astype(jnp.float32) / math.sqrt(CROSS_HEAD_DIM)
    p = jax.nn.softmax(s, axis=-1)
    o = jnp.einsum('bhsm,bmhd->bshd', p.astype(vm.dtype), vm)
    return o.reshape(B, S, CROSS_W)


def setup_inputs(seed: int = 0) -> dict:
    key = jax.random.key(seed)
    ks = jax.random.split(key, 20)
    nrm = lambda k, shape, fan_in: jax.random.normal(k, shape, jnp.float32) * (fan_in ** -0.5)
    x = jax.random.normal(ks[0], (BATCH, SEQ, D_MODEL), jnp.float32)
    mem = jax.random.normal(ks[1], (BATCH, MEM_LEN, D_MODEL), jnp.float32)
    offset = jax.random.randint(ks[2], (BATCH, 1), 0, 1024, dtype=jnp.int32)
    positions = offset + jnp.arange(SEQ, dtype=jnp.int32)[None, :]
    return {
        "x": x,
        "mem": mem,
        "positions": positions,
        "norm_g": 1.0 + 0.02 * jax.random.normal(ks[3], (DEPTH, D_MODEL), jnp.float32),
        "w_in": nrm(ks[4], (DEPTH, D_MODEL, IN_W), D_MODEL),
        "attn_sinks": 0.5 * jax.random.normal(ks[5], (DEPTH, N_Q_HEADS), jnp.float32),
        "w_o_attn": nrm(ks[6], (DEPTH, ATTN_W, D_MODEL), ATTN_W),
        "b_glu": 0.02 * jax.random.normal(ks[7], (DEPTH, 2 * CONV_W), jnp.float32),
        "w_dw": nrm(ks[8], (DEPTH, CONV_K, 1, CONV_W), CONV_K),
        "b_dw": 0.02 * jax.random.normal(ks[9], (DEPTH, CONV_W), jnp.float32),
        "ln_g": 1.0 + 0.02 * jax.random.normal(ks[10], (DEPTH, CONV_W), jnp.float32),
        "ln_b": 0.02 * jax.random.normal(ks[11], (DEPTH, CONV_W), jnp.float32),
        "w_pw": nrm(ks[12], (DEPTH, CONV_W, D_MODEL), CONV_W),
        "b_pw": 0.02 * jax.random.normal(ks[13], (DEPTH, D_MODEL), jnp.float32),
        "mem_norm_g": 1.0 + 0.02 * jax.random.normal(ks[14], (DEPTH, D_MODEL), jnp.float32),
        "w_mem_kv": nrm(ks[15], (DEPTH, D_MODEL, 2 * CROSS_W), D_MODEL),
        "w_o_cross": nrm(ks[16], (DEPTH, CROSS_W, D_MODEL), CROSS_W),
        "w_out": nrm(ks[17], (DEPTH, D_MODEL, D_MODEL), D_MODEL),
        "final_norm_g": 1.0 + 0.02 * jax.random.normal(ks[18], (D_MODEL,), jnp.float32),
    }


def reference(x, mem, positions, norm_g, w_in, attn_sinks, w_o_attn, b_glu, w_dw, b_dw,
              ln_g, ln_b, w_pw, b_pw, mem_norm_g, w_mem_kv, w_o_cross, w_out, final_norm_g):
    B, S = x.shape[0], x.shape[1]
    for l in range(DEPTH):
        h = rmsnorm(x, norm_g[l])
        proj = h @ w_in[l]
        (q, k, v, g_attn, u_conv, g_conv, q_cross, g_cross, merge) = jnp.split(proj, SPLIT_POINTS, axis=-1)
        a = sliding_window_attention(q.reshape(B, S, N_Q_HEADS, HEAD_DIM),
                                     k.reshape(B, S, N_KV_HEADS, HEAD_DIM),
                                     v.reshape(B, S, N_KV_HEADS, HEAD_DIM),
                                     positions, attn_sinks[l])
        y_attn = (a * jax.nn.silu(g_attn)) @ w_o_attn[l]
        c = conformer_conv(u_conv, b_glu[l], w_dw[l], b_dw[l], ln_g[l], ln_b[l])
        y_conv = (c * jax.nn.silu(g_conv)) @ w_pw[l] + b_pw[l]
        mem_n = rmsnorm(mem, mem_norm_g[l])
        m = memory_cross_attention(q_cross, mem_n, w_mem_kv[l])
        y_cross = (m * jax.nn.silu(g_cross)) @ w_o_cross[l]
        gates = jax.nn.sigmoid(merge).reshape(B, S, N_BRANCH, D_MODEL)
        merged = gates[:, :, 0] * y_attn + gates[:, :, 1] * y_conv + gates[:, :, 2] * y_cross
        x = x + merged @ w_out[l]
    return rmsnorm(x, final_norm_g)
```

```python
import contextlib
import math
import numpy as np
import concourse.bass as bass
import concourse.mybir as mybir
from concourse.bass_utils import run_bass_kernel_spmd

DT = mybir.dt
ALU = mybir.AluOpType
AF = mybir.ActivationFunctionType
F32 = DT.float32
BF = DT.bfloat16
I32 = DT.int32
ESZ = {DT.float32: 4, DT.bfloat16: 2, DT.int32: 4}

D_MODEL = 1024
BATCH = 4
SEQ = 4096
NCORES = 8
NTOK = 2048
HALO = 128
NT = NTOK + HALO
KC = 8
CONV_K = 31
CH = CONV_K - 1
RMS_EPS = 1e-6
LN_EPS = 1e-5
SLOPES = [2.0 ** (-8.0 * (i + 1) / 8) for i in range(8)]
BIG = 1.0e30
TG = 512
NTG = NTOK // TG

C_GX = 0
C_GM = 8
C_BGLU = 16
C_WDW = 24
C_BDW = C_WDW + 124
C_LNG = C_BDW + 4
C_LNB = C_LNG + 4
C_BPW = C_LNB + 4
C_SINK = C_BPW + 8
C_HV = C_SINK + 4
C_FG = C_HV + 1
C_WR = C_FG + 1024
NCONST = C_WR + 128
NPOS = 128 + 128 + 3


def _ap_runs(ap):
    es = ESZ[ap.dtype]
    pat = [list(x) for x in ap.ap]
    pstep = pat[0][0]
    off = ap.offset % pstep if pstep > 0 else ap.offset
    free = pat[1:]
    if not free:
        return [(off * es, (off + 1) * es)]
    runs = [off]
    for step, cnt in free[:-1]:
        runs = [r + i * step for r in runs for i in range(cnt)]
    step, cnt = free[-1]
    return [(r * es, (r + (cnt - 1) * abs(step) + 1) * es) for r in runs]


class _Op:
    __slots__ = ("idx", "eng", "emit", "deps", "signal", "sigval", "sem", "inc")

    def __init__(self, idx, eng, emit):
        self.idx = idx
        self.eng = eng
        self.emit = emit
        self.deps = set()
        self.signal = False
        self.sigval = None
        self.sem = None
        self.inc = 1


class Prog:
    PAGE = 64

    def __init__(self, nc):
        self.nc = nc
        self.ops = []
        self.pages = {}
        self.final_waits = []

    def _keys(self, ap):
        t = ap.tensor
        if type(t).__name__.startswith("DRam"):
            return []
        base = t.name
        if type(t).__name__.startswith("PSum"):
            page = 2048
            quads = (0,)
        else:
            page = self.PAGE
            pat = ap.ap
            pstep, pcnt = pat[0][0], pat[0][1]
            p0 = (ap.offset // pstep) if pstep > 0 else 0
            quads = tuple(range(p0 // 32, (p0 + pcnt - 1) // 32 + 1))
        ks = []
        for lo, hi in _ap_runs(ap):
            for p in range(lo // page, (hi - 1) // page + 1):
                for q in quads:
                    ks.append((base, p, q))
        return ks

    def add(self, eng, emit, reads=(), writes=(), dsem=None, final=False):
        idx = len(self.ops)
        op = _Op(idx, eng, emit)
        if dsem is not None:
            op.sem = dsem
            op.inc = 16
            op.signal = True
        deps = set()
        pages = self.pages
        for ap in reads:
            is_psum = type(ap.tensor).__name__.startswith("PSum")
            for k in self._keys(ap):
                rec = pages.get(k)
                if rec is None:
                    rec = pages[k] = [None, []]
                if rec[0] is not None:
                    deps.add(rec[0])
                if is_psum:
                    for r in rec[1]:
                        if r != idx and self.ops[r].eng != eng:
                            deps.add(r)
                rec[1].append(idx)
        for ap in writes:
            for k in self._keys(ap):
                rec = pages.get(k)
                if rec is None:
                    rec = pages[k] = [None, []]
                if rec[0] is not None:
                    deps.add(rec[0])
                for r in rec[1]:
                    deps.add(r)
                rec[0] = idx
                rec[1] = []
        deps.discard(idx)
        best = {}
        for d in deps:
            p = self.ops[d]
            if p.sem is not None:
                op.deps.add(d)
                continue
            if p.eng == eng and eng == "pe":
                continue
            if d > best.get(p.eng, -1):
                best[p.eng] = d
        for d in best.values():
            op.deps.add(d)
            self.ops[d].signal = True
        self.ops.append(op)
        if final:
            self.final_waits.append(idx)
        return idx

    def emit_all(self):
        nc = self.nc
        counts = {}
        for op in self.ops:
            if op.signal:
                s = op.sem if op.sem is not None else op.eng
                counts[s] = counts.get(s, 0) + op.inc
                op.sigval = counts[s]
        sem_names = sorted(counts.keys())
        with contextlib.ExitStack() as st:
            sems = {n: st.enter_context(nc.semaphore("s_" + n)) for n in sem_names}
            block = st.enter_context(nc.Block())
            ops = self.ops
            final_waits = self.final_waits

            def run(engname, e):
                seen = {}
                for op in ops:
                    if op.eng != engname:
                        continue
                    need = {}
                    for d in op.deps:
                        p = ops[d]
                        s = p.sem if p.sem is not None else p.eng
                        if p.sigval > need.get(s, 0):
                            need[s] = p.sigval
                    for s, v in need.items():
                        if seen.get(s, 0) >= v:
                            continue
                        e.wait_ge(sems[s], v)
                        seen[s] = v
                    ins = op.emit(e)
                    if op.signal:
                        s = op.sem if op.sem is not None else op.eng
                        ins.then_inc(sems[s], op.inc)
                if engname == "sp":
                    for fi in final_waits:
                        p = ops[fi]
                        s = p.sem
                        if seen.get(s, 0) < p.sigval:
                            e.wait_ge(sems[s], p.sigval)
                            seen[s] = p.sigval

            @block.tensor
            def _(e):
                run("pe", e)

            @block.scalar
            def _(e):
                run("act", e)

            @block.vector
            def _(e):
                run("dve", e)

            @block.gpsimd
            def _(e):
                run("pool", e)

            @block.sync
            def _(e):
                run("sp", e)


class Arena:
    def __init__(self, nc, nbytes):
        self.t = nc.alloc_sbuf_tensor("arena", [128, nbytes // 4], F32)
        self.nbytes = nbytes
        self.cur = 0
        self.limit = nbytes

    def view(self, off, shape, dtype):
        es = ESZ[dtype]
        n = 1
        for s in shape:
            n *= s
        nb = n * es
        assert off % 4 == 0 and nb % 4 == 0, (off, nb)
        assert off + nb <= self.nbytes, (off, nb, self.nbytes)
        ap = self.t[:, off // 4:(off + nb) // 4]
        if dtype != F32:
            ap = ap.bitcast(dtype)
        if len(shape) == 2:
            ap = ap.rearrange("p (a b) -> p a b", a=shape[0])
        elif len(shape) == 3:
            ap = ap.rearrange("p (a b c) -> p a b c", a=shape[0], b=shape[1])
        return ap

    def alloc(self, shape, dtype):
        es = ESZ[dtype]
        n = 1
        for s in shape:
            n *= s
        nb = (n * es + 63) // 64 * 64
        off = self.cur
        assert off + nb <= self.limit, ("arena overflow", off, nb, self.limit)
        self.cur += nb
        return self.view(off, shape, dtype)


def build_program(stage=5, debug=False):
    nc = bass.Bass("TRN2", target_bir_lowering=False)
    dram = {}

    def din(name, shape, dtype=F32):
        dram[name] = nc.dram_tensor(name, list(shape), dtype, kind="ExternalInput").ap()
        return dram[name]

    xT_d = din("xT", [KC, 128, NT])
    xtok_d = din("xtok", [NTOK, D_MODEL])
    memT_d = din("memT", [KC, 128, 256])
    consts_d = din("consts", [128, NCONST])
    pos_d = din("pos", [128, NPOS], I32)
    win_d = din("win", [14, 128, KC, 512])
    wo3_d = din("wo3", [3, 128, 4, 1024])
    wout_d = din("wout", [128, KC, 1024])
    wmem_d = din("wmem", [128, KC, 1024])
    out_d = nc.dram_tensor("out", [NTOK, D_MODEL], F32, kind="ExternalOutput").ap()
    dbg_out = {}

    A = Arena(nc, 204800)
    psum = nc.alloc_psum_tensor("psum", [128, 4096], F32)

    def bank(b, lo=0, hi=512):
        return psum[:, b * 512 + lo:b * 512 + hi]

    P = Prog(nc)

    hT = A.alloc([KC, NT], BF)
    gA = A.alloc([4, NTOK], BF)
    gC_off = A.cur
    gC = A.alloc([4, NTOK], BF)
    gX_off = A.cur
    gX = A.alloc([4, NTOK], BF)
    wslot = [A.alloc([KC, 512], BF) for _ in range(4)]
    consts = A.alloc([NCONST], F32)
    ident = A.alloc([128], BF)
    ones_bf = A.alloc([128], BF)
    onesm_bf = A.alloc([128], BF)
    ones_f = A.alloc([128], F32)
    esink = A.alloc([4], F32)
    E32 = A.alloc([32], BF)
    hb2 = A.alloc([4], F32)
    kmT = A.alloc([4, 256], BF)
    vm = A.alloc([2, 512], BF)
    nhvB = A.alloc([2], F32)
    epsT = A.alloc([2], F32)
    SCR = A.cur
    SCR_END = A.nbytes

    def scratch_reset(base=None, limit=None):
        A.cur = SCR if base is None else base
        A.limit = SCR_END if limit is None else limit

    def ccol(c, n=1):
        return consts[:, c:c + n]

    def mm(out, lhsT, rhs, start, stop, tile_position=None):
        if tile_position is None:
            P.add("pe", lambda e: e.matmul(out, lhsT=lhsT, rhs=rhs, start=start, stop=stop),
                  reads=[lhsT, rhs], writes=[out])
        else:
            P.add("pe", lambda e: e.matmul(out, lhsT=lhsT, rhs=rhs, start=start, stop=stop, tile_position=tile_position),
                  reads=[lhsT, rhs], writes=[out])

    def act(out, in_, func, bias=None, scale=None, accum_out=None, extra_reads=()):
        kw = {}
        rd = [in_] + list(extra_reads)
        if bias is not None:
            kw["bias"] = bias
            if not isinstance(bias, float):
                rd.append(bias)
        if scale is not None:
            kw["scale"] = scale
            if not isinstance(scale, float):
                rd.append(scale)
        wr = [out]
        if accum_out is not None:
            kw["accum_out"] = accum_out
            wr.append(accum_out)
        P.add("act", lambda e: e.activation(out=out, in_=in_, func=func, **kw), reads=rd, writes=wr)

    def tt(eng, out, in0, in1, op):
        P.add(eng, lambda e: e.tensor_tensor(out=out, in0=in0, in1=in1, op=op), reads=[in0, in1], writes=[out])

    def ts(eng, out, in0, s1, op0, s2=None, op1=None):
        rd = [in0]
        if not isinstance(s1, float):
            rd.append(s1)
        if s2 is not None and not isinstance(s2, float):
            rd.append(s2)
        if op1 is None:
            P.add(eng, lambda e: e.tensor_scalar(out=out, in0=in0, scalar1=s1, scalar2=None, op0=op0), reads=rd, writes=[out])
        else:
            P.add(eng, lambda e: e.tensor_scalar(out=out, in0=in0, scalar1=s1, scalar2=s2, op0=op0, op1=op1), reads=rd, writes=[out])

    def stt(eng, out, in0, scalar, in1, op0, op1):
        rd = [in0, in1]
        if not isinstance(scalar, float):
            rd.append(scalar)
        P.add(eng, lambda e: e.scalar_tensor_tensor(out=out, in0=in0, scalar=scalar, in1=in1, op0=op0, op1=op1),
              reads=rd, writes=[out])

    def copy(eng, out, in_):
        P.add(eng, lambda e: e.tensor_copy(out=out, in_=in_), reads=[in_], writes=[out])

    def memset(eng, out, val):
        P.add(eng, lambda e: e.memset(out, val), writes=[out])

    dma_n = [0]

    def dma(eng, out, in_, sem, final=False):
        rd = [in_]
        wr = [out]
        P.add(eng, lambda e: e.dma_start(out=out, in_=in_), reads=rd, writes=wr, dsem=sem, final=final)

    wl_n = [0]

    def wload(src):
        s = wl_n[0] % 4
        wl_n[0] += 1
        dma("pool", wslot[s], src, "w%d" % s)
        return wslot[s]

    def dump(name, ap):
        if not debug:
            return
        shp = [128, int(np.prod(ap.shape[1:]))]
        d = nc.dram_tensor("dbg_" + name, list(ap.shape), ap.dtype, kind="ExternalOutput").ap()
        dbg_out[name] = d
        dma("sp", d, ap, "dbg_" + name, final=True)

    dma("sp", consts, consts_d, "consts")
    scratch_reset()
    posi = A.alloc([NPOS], I32)
    posf = A.alloc([NPOS], F32)
    bias_prev = A.alloc([2, 512], BF)
    bias_cur = A.alloc([2, 512], BF)
    bias_halo = A.alloc([2, 512], BF)
    dtmp = [A.alloc([128], F32) for _ in range(4)]
    SCR1 = A.cur

    dma("sp", posi, pos_d, "pos")
    memset("pool", ones_f, 1.0)
    memset("pool", epsT[:, 0:1], RMS_EPS)
    memset("pool", epsT[:, 1:2], LN_EPS)
    memset("pool", ones_bf, 1.0)
    memset("pool", onesm_bf, 1.0 / 512.0)
    P.add("pool", lambda e: e.affine_select(out=ident, in_=ones_bf, pattern=[[-1, 128]], compare_op=ALU.is_equal,
                                             fill=0.0, base=0, channel_multiplier=1), reads=[ones_bf], writes=[ident])
    wl_n[0] = 1
    w_kv = wload(win_d[1])
    wl_n[0] = 0
    w_q = wload(win_d[0])
    wl_n[0] = 2
    w_ga = wload(win_d[2])
    wmem_sb = A.view(gC_off, [KC, 1024], BF)
    Lw = A.view(gX_off, [16, 8, 64], BF)

    copy("dve", posf, posi)
    ts("dve", hb2, ccol(C_BGLU + 4, 4), 0.5, ALU.mult)
    act(esink, ccol(C_SINK, 4), AF.Exp)
    ts("dve", nhvB[:, 0:1], ccol(C_HV), -BIG, ALU.mult, BIG, ALU.add)

    def build_bias(dst, qcols, kcol, halo):
        d, m1, dm, m2 = dtmp
        ts("dve", d, posf[:, qcols:qcols + 128], posf[:, kcol:kcol + 1], ALU.subtract)
        ts("dve", m1, d, 0.0, ALU.is_lt, BIG, ALU.mult)
        tt("dve", dm, m1, d, ALU.add)
        ts("dve", m2, d, 128.0, ALU.is_ge, BIG, ALU.mult)
        tt("dve", dm, dm, m2, ALU.add)
        if halo:
            ts("dve", dm, dm, nhvB[:, 0:1], ALU.add)
        for g in range(2):
            for j in range(4):
                ts("dve", dst[:, g, j * 128:(j + 1) * 128], dm, -8.0 * SLOPES[4 * g + j], ALU.mult)

    build_bias(bias_prev, 128, 257, False)
    build_bias(bias_cur, 128, 258, False)
    build_bias(bias_halo, 0, 256, True)

    A.cur = SCR1
    blocks = [(0, 640), (640, 1152), (1152, 1664), (1664, 2176)]
    xs = [A.alloc([KC, 640], F32), A.alloc([KC, 512], F32)]
    sq = [A.alloc([640], BF) for _ in range(2)]
    lnr = [A.alloc([640], F32), A.alloc([512], F32)]
    rstd = [A.alloc([640], F32), A.alloc([512], F32)]
    P0_END = A.cur
    for bi, (c0, c1) in enumerate(blocks):
        w = c1 - c0
        sl = bi % 2
        for kc in range(KC):
            o_, i_ = xs[sl][:, kc, 0:w], xT_d[kc, :, c0:c1]
            extra = [w_ga] if (bi == 1 and kc == 0) else []
            P.add("sp", lambda e, o_=o_, i_=i_: e.dma_start(out=o_, in_=i_), reads=[i_] + extra, writes=[o_],
                  dsem="xs%d_%d" % (sl, kc))
        pieces = [(0, 512), (512, w)] if w > 512 else [(0, w)]
        for kc in range(KC):
            act(sq[kc % 2][:, 0:w], xs[sl][:, kc, 0:w], AF.Square)
            for pi, (a, b) in enumerate(pieces):
                mm(bank(pi, 0, b - a), ones_bf, sq[kc % 2][:, a:b], kc == 0, kc == KC - 1)
        for pi, (a, b) in enumerate(pieces):
            act(lnr[sl][:, a:b], bank(pi, 0, b - a), AF.Ln, bias=epsT[:, 0:1], scale=1.0 / D_MODEL)
        act(rstd[sl][:, 0:w], lnr[sl][:, 0:w], AF.Exp, scale=-0.5)
        for kc in range(KC):
            stt("dve", hT[:, kc, c0:c1], xs[sl][:, kc, 0:w], ccol(C_GX + kc), rstd[sl][:, 0:w], ALU.mult, ALU.mult)
    dump("hT", hT)

    def inproj_fm(ps, wsl, unit, c0, n):
        for kc in range(KC):
            mm(ps, wsl[:, kc, unit * 128:(unit + 1) * 128], hT[:, kc, c0:c0 + n], kc == 0, kc == KC - 1)

    pjn = [0]
    pj = [6, 7]

    def next_pj():
        b = pj[pjn[0] % len(pj)]
        pjn[0] += 1
        return b

    if stage >= 1:
        A.cur = SCR1
        qT = [[A.alloc([4, TG], BF) for _ in range(2)] for _ in range(2)]
        sgT = [A.alloc([4, TG], BF) for _ in range(2)]
        kT = A.alloc([2, NT], BF)
        vtok = A.alloc([17, 128], BF)
        expT = [[[A.alloc([512], BF) for _ in range(2)] for _ in range(2)] for _ in range(2)]
        lsum = A.alloc([4, 128], F32)
        rinv = A.alloc([4, 128], F32)
        tmul = A.alloc([4, 128], F32)
        A.cur = max(A.cur, P0_END)
        memT = A.alloc([KC, 256], F32)
        sqm = [A.alloc([256], BF) for _ in range(2)]
        lnm = A.alloc([256], F32)
        rm = lnm
        memn = A.alloc([KC, 256], BF)
        assert A.cur <= SCR_END, (A.cur, SCR_END)
        for kc in range(KC):
            dma("sp", memT[:, kc, :], memT_d[kc], "memT%d" % kc)
        for sl_ in range(2):
            for hf_ in range(2):
                memset("pool", qT[sl_][hf_], 0.0)
        dma("pool", wmem_sb[:, :, 0:512], wmem_d[:, :, 0:512], "wmemA")
        dma("pool", wmem_sb[:, :, 512:1024], wmem_d[:, :, 512:1024], "wmemB")
        tt("pool", E32, ident[:, 0:32], ident[:, 32:64], ALU.add)
        tt("pool", E32, E32, ident[:, 64:96], ALU.add)
        tt("pool", E32, E32, ident[:, 96:128], ALU.add)
        memset("pool", Lw, 0.0)
        for cg in range(4, 16):
            for jj in range(8):
                h_ = (cg % 2) * 32
                ts("pool", Lw[:, cg, jj, h_:h_ + 32], E32, ccol(C_WR + cg * 8 + jj), ALU.mult)

        def p1_units(tg):
            T0 = tg * TG
            c0 = HALO + T0
            sl = tg % 2
            units = []

            def u_khalo(g):
                b = next_pj()
                inproj_fm(bank(b, 0, 128), w_kv, g, 0, 128)
                copy("dve", kT[:, g, 0:128], bank(b, 0, 128))

            def u_vhalo():
                b = next_pj()
                for kc in range(KC):
                    mm(bank(b, 0, 128), hT[:, kc, 0:128], w_kv[:, kc, 256:384], kc == 0, kc == KC - 1)
                copy("dve", vtok[:, 0, :], bank(b, 0, 128))

            def u_k(g):
                b = next_pj()
                inproj_fm(bank(b), w_kv, g, c0, TG)
                copy("dve", kT[:, g, c0:c0 + TG], bank(b))

            def u_v(ti):
                b = next_pj()
                tile = 1 + tg * 4 + ti
                for kc in range(KC):
                    mm(bank(b, 0, 128), hT[:, kc, c0 + ti * 128:c0 + (ti + 1) * 128], w_kv[:, kc, 256:384], kc == 0, kc == KC - 1)
                copy("dve", vtok[:, tile, :], bank(b, 0, 128))

            def u_q(c):
                b = next_pj()
                inproj_fm(bank(b), w_q, c, c0, TG)
                copy("dve", qT[sl][0][0:64, c, :], bank(b)[0:64, :])
                copy("dve", qT[sl][1][64:128, c, :], bank(b)[64:128, :])

            def u_ga(c):
                b = next_pj()
                inproj_fm(bank(b), w_ga, c, c0, TG)
                act(sgT[sl][:, c, :], bank(b), AF.Silu)

            g0 = []
            if tg == 0:
                g0 += [lambda g=g: u_khalo(g) for g in range(2)] + [u_vhalo]
            g0 += [lambda g=g: u_k(g) for g in range(2)] + [lambda t=t: u_v(t) for t in range(4)]
            g1 = [lambda c=c: u_q(c) for c in range(4)]
            g2 = [lambda c=c: u_ga(c) for c in range(4)]
            return [g0, g1, g2]

        def p1_scores(B):
            tg, bl = B // 4, B % 4
            sl = tg % 2
            Tq = tg * TG + bl * 128
            es = B % 2
            for g in range(2):
                for kcb in range(2):
                    sb = g * 2 + kcb
                    if kcb == 1:
                        bsrc = bias_cur
                    else:
                        bsrc = bias_halo if B == 0 else bias_prev
                    mm(bank(sb), ident, bsrc[:, g, :], True, False)
                    kcol = Tq + kcb * 128
                    for j in range(4):
                        cq = 2 * g + j // 2
                        hf = j % 2
                        mm(bank(sb, j * 128, (j + 1) * 128), kT[:, g, kcol:kcol + 128],
                           qT[sl][hf][:, cq, bl * 128:(bl + 1) * 128], False, j == 3)
                    act(expT[es][g][kcb], bank(sb), AF.Exp, scale=0.125)

        def p1_pv(B):
            tg, bl = B // 4, B % 4
            sl = tg % 2
            Tq = tg * TG + bl * 128
            es = B % 2
            for bk, lhs_of in ((4, None), (5, ones_bf[:, 0:64])):
                for cq in range(4):
                    for hf in range(2):
                        h = 2 * cq + hf
                        g = h // 4
                        j = h % 4
                        for kcb in range(2):
                            lhsT = vtok[:, B + kcb, g * 64:(g + 1) * 64] if lhs_of is None else lhs_of
                            mm(bank(bk, cq * 128, (cq + 1) * 128)[hf * 64:(hf + 1) * 64, :], lhsT,
                               expT[es][g][kcb][:, j * 128:(j + 1) * 128], kcb == 0, kcb == 1)
            for cq in range(4):
                act(lsum[:, cq, :], bank(5, cq * 128, (cq + 1) * 128), AF.Ln, bias=esink[:, cq:cq + 1])
            act(rinv, lsum, AF.Exp, scale=-1.0)
            tt("dve", tmul, rinv, sgT[sl][:, :, bl * 128:(bl + 1) * 128], ALU.mult)
            tt("dve", gA[:, :, Tq:Tq + 128], bank(4).rearrange("p (a b) -> p a b", a=4), tmul, ALU.mult)

        def mem_stats():
            b = next_pj()
            for kc in range(KC):
                act(sqm[kc % 2], memT[:, kc, :], AF.Square)
                mm(bank(b, 0, 256), ones_bf, sqm[kc % 2], kc == 0, kc == KC - 1)
            act(lnm, bank(b, 0, 256), AF.Ln, bias=epsT[:, 0:1], scale=1.0 / D_MODEL)
            act(rm, lnm, AF.Exp, scale=-0.5)
            for kc in range(KC):
                stt("dve", memn[:, kc, :], memT[:, kc, :], ccol(C_GM + kc), rm, ALU.mult, ALU.mult)

        def mem_kv():
            for hd in range(4):
                b = next_pj()
                for kc in range(KC):
                    mm(bank(b, 0, 256), wmem_sb[:, kc, hd * 128:(hd + 1) * 128], memn[:, kc, :], kc == 0, kc == KC - 1)
                copy("dve", kmT[:, hd, :], bank(b, 0, 256))
            for mc in range(2):
                b = next_pj()
                for kc in range(KC):
                    mm(bank(b), memn[:, kc, mc * 128:(mc + 1) * 128], wmem_sb[:, kc, 512:1024], kc == 0, kc == KC - 1)
                copy("dve", vm[:, mc, :], bank(b))

        NB = NTG * 4
        for grp in p1_units(0):
            for u in grp:
                u()
        pending = None
        nxt = None
        for B in range(NB):
            tg, bl = B // 4, B % 4
            if bl == 0:
                nxt = p1_units(tg + 1) if tg + 1 < NTG else None
            p1_scores(B)
            if pending is not None:
                p1_pv(pending)
            pending = B
            if nxt is not None and bl < 3:
                for u in nxt[bl]:
                    u()
            if B == 1:
                mem_stats()
            if B == 3:
                mem_kv()
            if tg == NTG - 1 and bl == 0:
                w_ul = wload(win_d[3])
                w_ug = wload(win_d[4])
                w_gc = wload(win_d[5])
        p1_pv(pending)
        dump("gA", gA)
        dump("kT", kT)
        dump("vtok", vtok)

    if stage >= 2:
        CP = 32
        A.cur = SCR_END - (4 * (CP + NTOK) * 2 + 2 * TG * 4)
        P2_TOP = A.cur
        aT = A.alloc([4, CP + NTOK], BF)
        sig = [A.alloc([TG], F32) for _ in range(2)]
        A.cur = SCR
        A.limit = P2_TOP
        RW = 544
        Rt = A.alloc([4, 4, RW], BF)
        cfp = A.alloc([4, TG], F32)
        cbf = A.alloc([4, TG], BF)
        csq = A.alloc([4, TG], BF)
        m2 = A.alloc([TG], F32)
        var = A.alloc([TG], F32)
        lnv = var
        rsl = A.alloc([TG], F32)
        ytmp = [A.alloc([TG], F32) for _ in range(2)]
        stmp = [A.alloc([TG], BF) for _ in range(2)]
        sgc = [A.alloc([4, TG], BF) for _ in range(2)]
        A.limit = SCR_END
        memset("pool", Rt[:, :, :, RW - 8:RW], 0.0)

        pj = [4, 5, 6, 7]
        sgn = [0]

        def p2_inproj_steps(tg):
            T0 = tg * TG
            c0 = HALO + T0
            sl = tg % 2
            s0 = CP - CH + T0

            def halo():
                for c in range(4):
                    b = next_pj()
                    inproj_fm(bank(b, 0, CH), w_ug, c, HALO - CH, CH)
                    s_ = sig[sgn[0] % 2]
                    sgn[0] += 1
                    act(s_[:, 0:CH], bank(b, 0, CH), AF.Tanh, bias=hb2[:, c:c + 1], scale=0.5)
                    ts("dve", s_[:, 0:CH], s_[:, 0:CH], 1.0, ALU.add, 0.5, ALU.mult)
                    b2 = next_pj()
                    inproj_fm(bank(b2, 0, CH), w_ul, c, HALO - CH, CH)
                    stt("dve", ytmp[0][:, 0:CH], bank(b2, 0, CH), ccol(C_BGLU + c), s_[:, 0:CH], ALU.add, ALU.mult)
                    ts("dve", aT[:, c, CP - CH:CP], ytmp[0][:, 0:CH], ccol(C_HV), ALU.mult)

            def glu(c):
                b = next_pj()
                inproj_fm(bank(b), w_ug, c, c0, TG)
                s_ = sig[sgn[0] % 2]
                sgn[0] += 1
                act(s_, bank(b), AF.Tanh, bias=hb2[:, c:c + 1], scale=0.5)
                ts("dve", s_, s_, 1.0, ALU.add, 0.5, ALU.mult)
                b2 = next_pj()
                inproj_fm(bank(b2), w_ul, c, c0, TG)
                stt("dve", aT[:, c, CP + T0:CP + T0 + TG], bank(b2), ccol(C_BGLU + c), s_, ALU.add, ALU.mult)

            def relayout(cp):
                for m in range(4):
                    for r in range(4):
                        L_ = 540 if r < 3 else 539
                        dma("sp", Rt[r * 32:(r + 1) * 32, 2 * cp:2 * cp + 2, m, 0:L_],
                            aT[m * 32:(m + 1) * 32, 2 * cp:2 * cp + 2, s0 + r:s0 + r + L_], "R%d_%d_%d" % (cp, m, r))

            def gc(c):
                b = next_pj()
                inproj_fm(bank(b), w_gc, c, c0, TG)
                act(sgc[sl][:, c, :], bank(b), AF.Silu)

            st = {"glu": [lambda c=c: glu(c) for c in range(4)], "rel": [lambda cp=cp: relayout(cp) for cp in range(2)],
                  "gc": [lambda c=c: gc(c) for c in range(4)], "halo": halo}
            return st

        def p2_conv(tg):
            T0 = tg * TG
            for c in range(4):
                cb = c % 2
                for jj in range(8):
                    for m in (0, 2, 1, 3):
                        h2 = m // 2
                        mm(bank(cb)[h2 * 64:(h2 + 1) * 64, :], Lw[:, 4 * c + m, jj, :], Rt[:, c, m, 4 * jj:4 * jj + TG],
                           jj == 0 and m % 2 == 0, jj == 7 and m % 2 == 1, tile_position=(0, 64 * h2))
                ts("dve", cfp[:, c, :], bank(cb), ccol(C_BDW + c), ALU.add)
                ts("dve", cbf[:, c, :], bank(cb), ccol(C_BDW + c), ALU.add)
                act(csq[:, c, :], cfp[:, c, :], AF.Square)

        def p2_ln_steps(tg):
            T0 = tg * TG
            sl = tg % 2

            def stats():
                for c in range(4):
                    mm(bank(2), onesm_bf, cbf[:, c, :], c == 0, c == 3)
                for c in range(4):
                    mm(bank(3), onesm_bf, csq[:, c, :], c == 0, c == 3)

            def head():
                act(m2, bank(2), AF.Square)
                tt("dve", var, bank(3), m2, ALU.subtract)
                act(lnv, var, AF.Ln, bias=epsT[:, 1:2])
                act(rsl, lnv, AF.Exp, scale=-0.5)

            def chunk(c):
                y = ytmp[c % 2]
                s2 = stmp[c % 2]
                tt("dve", y, cfp[:, c, :], bank(2), ALU.subtract)
                tt("dve", y, y, rsl, ALU.mult)
                act(s2, y, AF.Silu, bias=ccol(C_LNB + c), scale=ccol(C_LNG + c))
                tt("dve", gC[:, c, T0:T0 + TG], s2, sgc[sl][:, c, :], ALU.mult)

            return {"stats": stats, "head": head, "chunk": [lambda c=c: chunk(c) for c in range(4)]}

        qcT = [A.view(SCR + i * 4096, [4, TG], BF) for i in range(2)]
        sgx = [A.view(SCR + 8192 + i * 4096, [4, TG], BF) for i in range(2)]
        p3_started = [False]

        def p3_units(tg):
            T0 = tg * TG
            c0 = HALO + T0
            sl = tg % 2

            def u_q(hd):
                b = next_pj()
                inproj_fm(bank(b), w_qc_[0], hd, c0, TG)
                copy("dve", qcT[sl][:, hd, :], bank(b))

            def u_g(hd):
                b = next_pj()
                inproj_fm(bank(b), w_gx_[0], hd, c0, TG)
                act(sgx[sl][:, hd, :], bank(b), AF.Silu)

            return [lambda h=h: u_q(h) for h in range(4)] + [lambda h=h: u_g(h) for h in range(4)]

        w_qc_ = [None]
        w_gx_ = [None]
        st0 = p2_inproj_steps(0)
        for f in st0["glu"]:
            f()
        st0["halo"]()
        st0["rel"][0]()
        st0["rel"][1]()
        for f in st0["gc"]:
            f()
        for cg in range(4):
            for jj in range(8):
                h_ = (cg % 2) * 32
                ts("dve", Lw[:, cg, jj, h_:h_ + 32], E32, ccol(C_WR + cg * 8 + jj), ALU.mult)
        for tg in range(NTG):
            p2_conv(tg)
            ln = p2_ln_steps(tg)
            if tg + 1 < NTG:
                nx = p2_inproj_steps(tg + 1)
                slots = [[nx["glu"][0]], [nx["glu"][1], nx["rel"][0]], [], [nx["glu"][2]], [nx["glu"][3], nx["rel"][1]],
                         [nx["gc"][0]], [nx["gc"][1], nx["gc"][2], nx["gc"][3]]]
            elif stage >= 3:
                pu = p3_units(0)
                slots = [[pu[0]], [pu[1]], [], [pu[2]], [pu[3], pu[4]], [pu[5]], [pu[6], pu[7]]]
                p3_started[0] = True
            else:
                slots = [[], [], [], [], [], [], []]
            lsteps = [ln["stats"], ln["head"]] + ln["chunk"] + [lambda: None]
            for k in range(7):
                for f in slots[k]:
                    f()
                lsteps[k]()
            if tg == NTG - 2:
                w_qc_[0] = wload(win_d[6])
                w_gx_[0] = wload(win_d[7])
        dump("gC", gC)
        pass

    if stage >= 3:
        A.cur = SCR + 4 * TG * 2 * 4
        ex = [[A.alloc([TG], BF) for _ in range(2)] for _ in range(2)]
        lx = A.alloc([TG], F32)
        rx = A.alloc([TG], F32)
        tx = A.alloc([TG], F32)
        assert A.cur <= SCR + 49152
        A.cur = SCR + 49152
        wo3 = [A.alloc([4, 1024], BF) for _ in range(3)]
        pj = [6, 7]
        inv_sqrt_d = 1.0 / math.sqrt(128.0)

        def p3_scores(H):
            tg, hd = H // 4, H % 4
            sl = tg % 2
            es = hd % 2
            for mc in range(2):
                sb = (hd % 2) * 2 + mc
                mm(bank(sb), kmT[:, hd, mc * 128:(mc + 1) * 128], qcT[sl][:, hd, :], True, True)
                act(ex[es][mc], bank(sb), AF.Exp, scale=inv_sqrt_d)

        def p3_pv(H):
            tg, hd = H // 4, H % 4
            sl = tg % 2
            es = hd % 2
            T0 = tg * TG
            for mc in range(2):
                mm(bank(4), vm[:, mc, hd * 128:(hd + 1) * 128], ex[es][mc], mc == 0, mc == 1)
            for mc in range(2):
                mm(bank(5), ones_bf, ex[es][mc], mc == 0, mc == 1)
            act(lx, bank(5), AF.Ln)
            act(rx, lx, AF.Exp, scale=-1.0)
            tt("dve", tx, rx, sgx[sl][:, hd, :], ALU.mult)
            tt("dve", gX[:, hd, T0:T0 + TG], bank(4), tx, ALU.mult)

        if not p3_started[0]:
            for u in p3_units(0):
                u()
        for i in range(3):
            dma("pool", wo3[i], wo3_d[i], "wo3_%d" % i)
        pending = None
        nxt = []
        for H in range(NTG * 4):
            tg, hd = H // 4, H % 4
            if hd == 0:
                nxt = p3_units(tg + 1) if tg + 1 < NTG else []
            p3_scores(H)
            if pending is not None:
                p3_pv(pending)
            pending = H
            for u in nxt[hd * 2:hd * 2 + 2]:
                u()
            if tg == NTG - 1 and hd == 0:
                wmg = [wload(win_d[8]), wload(win_d[9])]
        p3_pv(pending)
        dump("gX", gX)

    if stage >= 4:
        A.cur = SCR
        mergedT = A.alloc([KC, NTOK], BF)
        sg = [A.alloc([TG], F32) for _ in range(4)]
        tm = [A.alloc([TG], F32) for _ in range(4)]
        assert A.cur <= SCR + 49152
        gsrc = [gA, gC, gX]
        sgi = [0]
        it = 0
        for c in range(KC):
            need = min(5, (3 * min(c + 1, KC - 1) + 2) // 4)
            while len(wmg) <= need:
                wmg.append(wload(win_d[8 + len(wmg)]))
            if c == KC - 1:
                w_o = [wload(wout_d[:, :, 0:512]), wload(wout_d[:, :, 512:1024])]
            for tg in range(NTG):
                T0 = tg * TG
                c0 = HALO + T0
                for i in range(3):
                    u = 3 * c + i
                    bg = (it * 3 + i) % 4
                    inproj_fm(bank(bg), wmg[u // 4], u % 4, c0, TG)
                    s_ = sg[sgi[0] % 4]
                    sgi[0] += 1
                    act(s_, bank(bg), AF.Sigmoid)
                    by = 4 + (it * 3 + i) % 4
                    for k4 in range(4):
                        mm(bank(by), wo3[i][:, k4, c * 128:(c + 1) * 128], gsrc[i][:, k4, T0:T0 + TG], k4 == 0, k4 == 3)
                    t_ = tm[i]
                    if i == 1:
                        stt("dve", t_, bank(by), ccol(C_BPW + c), s_, ALU.add, ALU.mult)
                    else:
                        tt("dve", t_, bank(by), s_, ALU.mult)
                tt("pool", tm[3], tm[0], tm[1], ALU.add)
                tt("dve", mergedT[:, c, T0:T0 + TG], tm[3], tm[2], ALU.add)
                it += 1
        dump("mergedT", mergedT)

    if stage >= 5:
        A.cur = SCR + 32768
        NX, NZ = 3, 4
        xsb = [A.alloc([1024], F32) for _ in range(NX)]
        zsb = [A.alloc([1024], F32) for _ in range(NZ)]
        junk = A.alloc([1024], BF)
        ssq = A.alloc([16], F32)
        lnf = A.alloc([16], F32)
        rsf = A.alloc([16], F32)
        assert A.cur <= SCR_END

        def xload(t):
            dma("sp", xsb[t % NX], xtok_d[t * 128:(t + 1) * 128, :], "xtok%d" % (t % NX))

        for t in range(min(NX, 16)):
            xload(t)
        for t in range(16):
            zb = (t % 4) * 2
            for nh in range(2):
                for kc in range(KC):
                    mm(bank(zb + nh), mergedT[:, kc, t * 128:(t + 1) * 128], w_o[nh][:, kc, :], kc == 0, kc == KC - 1)
            zps = psum[:, zb * 512:(zb + 2) * 512]
            z = zsb[t % NZ]
            tt("dve", z, zps, xsb[t % NX], ALU.add)
            if t + NX < 16:
                xload(t + NX)
            act(junk, z, AF.Square, accum_out=ssq[:, t:t + 1])
            act(lnf[:, t:t + 1], ssq[:, t:t + 1], AF.Ln, bias=epsT[:, 0:1], scale=1.0 / D_MODEL)
            act(rsf[:, t:t + 1], lnf[:, t:t + 1], AF.Exp, scale=-0.5)
            stt("dve", z, z, rsf[:, t:t + 1], ccol(C_FG, 1024), ALU.mult, ALU.mult)
            dma("pool", out_d[t * 128:(t + 1) * 128, :], z, "out%d" % (t % NZ), final=True)

    P.emit_all()
    return nc, dbg_out


def _kmajor(w, kc):
    n = w.shape[1]
    return np.ascontiguousarray(w.reshape(kc, 128, n).transpose(1, 0, 2))


def _prep_inputs(x, mem, positions, norm_g, w_in, attn_sinks, w_o_attn, b_glu, w_dw, b_dw, ln_g, ln_b,
                 w_pw, b_pw, mem_norm_g, w_mem_kv, w_o_cross, w_out, final_norm_g):
    f32 = np.float32
    x = np.asarray(x, f32)
    mem = np.asarray(mem, f32)
    positions = np.asarray(positions, np.int32)
    W = np.asarray(w_in, f32)[0]
    cols = []
    cols.append(np.arange(0, 512))
    kd = []
    for g in range(2):
        kk = 512 + g * 64 + np.arange(64)
        kd += [kk, kk]
    cols.append(np.concatenate(kd + [640 + np.arange(128), 640 + np.arange(128)]))
    cols.append(768 + np.arange(512))
    cols.append(1280 + np.arange(512))
    cols.append(1792 + np.arange(512))
    cols.append(2304 + np.arange(512))
    cols.append(2816 + np.arange(512))
    cols.append(3328 + np.arange(512))
    mg = []
    for c in range(8):
        for i in range(3):
            mg.append(3840 + i * 1024 + c * 128 + np.arange(128))
    mg = np.concatenate(mg)
    for s in range(6):
        cols.append(mg[s * 512:(s + 1) * 512])
    win = np.stack([_kmajor(W[:, cc], KC) for cc in cols], axis=0)
    wo3 = np.stack([_kmajor(np.asarray(w, f32)[0], 4) for w in (w_o_attn, w_pw, w_o_cross)], axis=0)
    wout = _kmajor(np.asarray(w_out, f32)[0], KC)
    wmem = _kmajor(np.asarray(w_mem_kv, f32)[0], KC)

    def pcol(v, n):
        return np.asarray(v, f32).reshape(n, 128).T

    cbase = np.zeros((128, NCONST), f32)
    cbase[:, C_GX:C_GX + 8] = pcol(norm_g[0], 8)
    cbase[:, C_GM:C_GM + 8] = pcol(mem_norm_g[0], 8)
    cbase[:, C_BGLU:C_BGLU + 8] = pcol(b_glu[0], 8)
    wd = np.asarray(w_dw, f32)[0, :, 0, :]
    for c in range(4):
        cbase[:, C_WDW + c * CONV_K:C_WDW + (c + 1) * CONV_K] = wd[:, c * 128:(c + 1) * 128].T
    for cg in range(16):
        for jj in range(8):
            for r in range(4):
                j = 4 * jj + r
                if j < CONV_K:
                    cbase[r * 32:(r + 1) * 32, C_WR + cg * 8 + jj] = wd[j, cg * 32:(cg + 1) * 32]
    cbase[:, C_BDW:C_BDW + 4] = pcol(b_dw[0], 4)
    cbase[:, C_LNG:C_LNG + 4] = pcol(ln_g[0], 4)
    cbase[:, C_LNB:C_LNB + 4] = pcol(ln_b[0], 4)
    cbase[:, C_BPW:C_BPW + 8] = pcol(b_pw[0], 8)
    sk = np.asarray(attn_sinks, f32)[0]
    for c in range(4):
        cbase[0:64, C_SINK + c] = sk[2 * c]
        cbase[64:128, C_SINK + c] = sk[2 * c + 1]
    cbase[:, C_FG:C_FG + 1024] = np.asarray(final_norm_g, f32)[None, :]

    in_maps = []
    for core in range(NCORES):
        b = core // 2
        st = (core % 2) * NTOK
        xo = x[b, st:st + NTOK]
        if st > 0:
            xh = x[b, st - HALO:st]
            ph = positions[b, st - HALO:st]
            hv = 1.0
        else:
            xh = np.zeros((HALO, D_MODEL), f32)
            ph = np.zeros((HALO,), np.int32)
            hv = 0.0
        xcat = np.concatenate([xh, xo], axis=0)
        xT = np.ascontiguousarray(xcat.T).reshape(KC, 128, NT)
        memT = np.ascontiguousarray(mem[b].T).reshape(KC, 128, 256)
        cst = cbase.copy()
        cst[:, C_HV] = hv
        pos = np.zeros((128, NPOS), np.int32)
        pos[:, 0:128] = positions[b, st:st + 128][None, :]
        pos[:, 128:256] = positions[b, st + 128:st + 256][None, :]
        pos[:, 256] = ph
        pos[:, 257] = positions[b, st:st + 128]
        pos[:, 258] = positions[b, st + 128:st + 256]
        in_maps.append({"xT": xT, "xtok": np.ascontiguousarray(xo), "memT": memT, "consts": cst, "pos": pos,
                        "win": win, "wo3": wo3, "wout": wout, "wmem": wmem})
    return in_maps


_CACHE = {}


def kernel(**inputs):
    in_maps = _prep_inputs(**inputs)
    if "nc" not in _CACHE:
        _CACHE["nc"] = build_program(stage=5, debug=False)[0]
    nc = _CACHE["nc"]
    res = run_bass_kernel_spmd(nc, in_maps, core_ids=list(range(NCORES)))
    out = np.empty((BATCH, SEQ, D_MODEL), np.float32)
    for core in range(NCORES):
        b = core // 2
        st = (core % 2) * NTOK
        out[b, st:st + NTOK] = res.results[core]["out"]
    return out
```

```python
import contextlib
import math
import numpy as np
import concourse.bass as bass
import concourse.mybir as mybir
from concourse.bass_utils import run_bass_kernel_spmd

DT = mybir.dt
ALU = mybir.AluOpType
AF = mybir.ActivationFunctionType
F32 = DT.float32
BF = DT.bfloat16
I32 = DT.int32
ESZ = {DT.float32: 4, DT.bfloat16: 2, DT.int32: 4}

D_MODEL = 1024
BATCH = 4
SEQ = 4096
NCORES = 8
NTOK = 2048
HALO = 128
NT = NTOK + HALO
KC = 8
CONV_K = 31
CH = CONV_K - 1
RMS_EPS = 1e-6
LN_EPS = 1e-5
SLOPES = [2.0 ** (-8.0 * (i + 1) / 8) for i in range(8)]
BIG = 1.0e30
TG = 512
NTG = NTOK // TG

C_GX = 0
C_GM = 8
C_BGLU = 16
C_WDW = 24
C_BDW = C_WDW + 124
C_LNG = C_BDW + 4
C_LNB = C_LNG + 4
C_BPW = C_LNB + 4
C_SINK = C_BPW + 8
C_HV = C_SINK + 4
C_FG = C_HV + 1
C_WR = C_FG + 1024
NCONST = C_WR + 128
NPOS = 128 + 128 + 3


def _ap_runs(ap):
    es = ESZ[ap.dtype]
    pat = [list(x) for x in ap.ap]
    pstep = pat[0][0]
    off = ap.offset % pstep if pstep > 0 else ap.offset
    free = pat[1:]
    if not free:
        return [(off * es, (off + 1) * es)]
    runs = [off]
    for step, cnt in free[:-1]:
        runs = [r + i * step for r in runs for i in range(cnt)]
    step, cnt = free[-1]
    return [(r * es, (r + (cnt - 1) * abs(step) + 1) * es) for r in runs]


class _Op:
    __slots__ = ("idx", "eng", "emit", "deps", "signal", "sigval", "sem", "inc")

    def __init__(self, idx, eng, emit):
        self.idx = idx
        self.eng = eng
        self.emit = emit
        self.deps = set()
        self.signal = False
        self.sigval = None
        self.sem = None
        self.inc = 1


class Prog:
    PAGE = 64

    def __init__(self, nc):
        self.nc = nc
        self.ops = []
        self.pages = {}
        self.final_waits = []

    def _keys(self, ap):
        t = ap.tensor
        if type(t).__name__.startswith("DRam"):
            return []
        base = t.name
        if type(t).__name__.startswith("PSum"):
            page = 2048
            quads = (0,)
        else:
            page = self.PAGE
            pat = ap.ap
            pstep, pcnt = pat[0][0], pat[0][1]
            p0 = (ap.offset // pstep) if pstep > 0 else 0
            quads = tuple(range(p0 // 32, (p0 + pcnt - 1) // 32 + 1))
        ks = []
        for lo, hi in _ap_runs(ap):
            for p in range(lo // page, (hi - 1) // page + 1):
                for q in quads:
                    ks.append((base, p, q))
        return ks

    def add(self, eng, emit, reads=(), writes=(), dsem=None, final=False):
        idx = len(self.ops)
        op = _Op(idx, eng, emit)
        if dsem is not None:
            op.sem = dsem
            op.inc = 16
            op.signal = True
        deps = set()
        pages = self.pages
        for ap in reads:
            is_psum = type(ap.tensor).__name__.startswith("PSum")
            for k in self._keys(ap):
                rec = pages.get(k)
                if rec is None:
                    rec = pages[k] = [None, []]
                if rec[0] is not None:
                    deps.add(rec[0])
                if is_psum:
                    for r in rec[1]:
                        if r != idx and self.ops[r].eng != eng:
                            deps.add(r)
                rec[1].append(idx)
        for ap in writes:
            for k in self._keys(ap):
                rec = pages.get(k)
                if rec is None:
                    rec = pages[k] = [None, []]
                if rec[0] is not None:
                    deps.add(rec[0])
                for r in rec[1]:
                    deps.add(r)
                rec[0] = idx
                rec[1] = []
        deps.discard(idx)
        best = {}
        for d in deps:
            p = self.ops[d]
            if p.sem is not None:
                op.deps.add(d)
                continue
            if p.eng == eng and eng == "pe":
                continue
            if d > best.get(p.eng, -1):
                best[p.eng] = d
        for d in best.values():
            op.deps.add(d)
            self.ops[d].signal = True
        self.ops.append(op)
        if final:
            self.final_waits.append(idx)
        return idx

    def emit_all(self):
        nc = self.nc
        counts = {}
        for op in self.ops:
            if op.signal:
                s = op.sem if op.sem is not None else op.eng
                counts[s] = counts.get(s, 0) + op.inc
                op.sigval = counts[s]
        sem_names = sorted(counts.keys())
        with contextlib.ExitStack() as st:
            sems = {n: st.enter_context(nc.semaphore("s_" + n)) for n in sem_names}
            block = st.enter_context(nc.Block())
            ops = self.ops
            final_waits = self.final_waits

            def run(engname, e):
                seen = {}
                for op in ops:
                    if op.eng != engname:
                        continue
                    need = {}
                    for d in op.deps:
                        p = ops[d]
                        s = p.sem if p.sem is not None else p.eng
                        if p.sigval > need.get(s, 0):
                            need[s] = p.sigval
                    for s, v in need.items():
                        if seen.get(s, 0) >= v:
                            continue
                        e.wait_ge(sems[s], v)
                        seen[s] = v
                    ins = op.emit(e)
                    if op.signal:
                        s = op.sem if op.sem is not None else op.eng
                        ins.then_inc(sems[s], op.inc)
                if engname == "sp":
                    for fi in final_waits:
                        p = ops[fi]
                        s = p.sem
                        if seen.get(s, 0) < p.sigval:
                            e.wait_ge(sems[s], p.sigval)
                            seen[s] = p.sigval

            @block.tensor
            def _(e):
                run("pe", e)

            @block.scalar
            def _(e):
                run("act", e)

            @block.vector
            def _(e):
                run("dve", e)

            @block.gpsimd
            def _(e):
                run("pool", e)

            @block.sync
            def _(e):
                run("sp", e)


class Arena:
    def __init__(self, nc, nbytes):
        self.t = nc.alloc_sbuf_tensor("arena", [128, nbytes // 4], F32)
        self.nbytes = nbytes
        self.cur = 0
        self.limit = nbytes

    def view(self, off, shape, dtype):
        es = ESZ[dtype]
        n = 1
        for s in shape:
            n *= s
        nb = n * es
        assert off % 4 == 0 and nb % 4 == 0, (off, nb)
        assert off + nb <= self.nbytes, (off, nb, self.nbytes)
        ap = self.t[:, off // 4:(off + nb) // 4]
        if dtype != F32:
            ap = ap.bitcast(dtype)
        if len(shape) == 2:
            ap = ap.rearrange("p (a b) -> p a b", a=shape[0])
        elif len(shape) == 3:
            ap = ap.rearrange("p (a b c) -> p a b c", a=shape[0], b=shape[1])
        return ap

    def alloc(self, shape, dtype):
        es = ESZ[dtype]
        n = 1
        for s in shape:
            n *= s
        nb = (n * es + 63) // 64 * 64
        off = self.cur
        assert off + nb <= self.limit, ("arena overflow", off, nb, self.limit)
        self.cur += nb
        return self.view(off, shape, dtype)


def build_program(stage=5, debug=False):
    nc = bass.Bass("TRN2", target_bir_lowering=False)
    dram = {}

    def din(name, shape, dtype=F32):
        dram[name] = nc.dram_tensor(name, list(shape), dtype, kind="ExternalInput").ap()
        return dram[name]

    xT_d = din("xT", [KC, 128, NT])
    xtok_d = din("xtok", [NTOK, D_MODEL])
    memT_d = din("memT", [KC, 128, 256])
    consts_d = din("consts", [128, NCONST])
    pos_d = din("pos", [128, NPOS], I32)
    win_d = din("win", [14, 128, KC, 512])
    wo3_d = din("wo3", [3, 128, 4, 1024])
    wout_d = din("wout", [128, KC, 1024])
    wmem_d = din("wmem", [128, KC, 1024])
    out_d = nc.dram_tensor("out", [NTOK, D_MODEL], F32, kind="ExternalOutput").ap()
    dbg_out = {}

    A = Arena(nc, 204800)
    psum = nc.alloc_psum_tensor("psum", [128, 4096], F32)

    def bank(b, lo=0, hi=512):
        return psum[:, b * 512 + lo:b * 512 + hi]

    P = Prog(nc)

    hT = A.alloc([KC, NT], BF)
    gA = A.alloc([4, NTOK], BF)
    gC_off = A.cur
    gC = A.alloc([4, NTOK], BF)
    gX_off = A.cur
    gX = A.alloc([4, NTOK], BF)
    wslot = [A.alloc([KC, 512], BF) for _ in range(4)]
    consts = A.alloc([NCONST], F32)
    ident = A.alloc([128], BF)
    ones_bf = A.alloc([128], BF)
    onesm_bf = A.alloc([128], BF)
    ones_f = A.alloc([128], F32)
    esink = A.alloc([4], F32)
    E32 = A.alloc([32], BF)
    hb2 = A.alloc([4], F32)
    kmT = A.alloc([4, 256], BF)
    vm = A.alloc([2, 512], BF)
    nhvB = A.alloc([2], F32)
    epsT = A.alloc([2], F32)
    SCR = A.cur
    SCR_END = A.nbytes

    def scratch_reset(base=None, limit=None):
        A.cur = SCR if base is None else base
        A.limit = SCR_END if limit is None else limit

    def ccol(c, n=1):
        return consts[:, c:c + n]

    def mm(out, lhsT, rhs, start, stop, tile_position=None):
        if tile_position is None:
            P.add("pe", lambda e: e.matmul(out, lhsT=lhsT, rhs=rhs, start=start, stop=stop),
                  reads=[lhsT, rhs], writes=[out])
        else:
            P.add("pe", lambda e: e.matmul(out, lhsT=lhsT, rhs=rhs, start=start, stop=stop, tile_position=tile_position),
                  reads=[lhsT, rhs], writes=[out])

    def act(out, in_, func, bias=None, scale=None, accum_out=None, extra_reads=()):
        kw = {}
        rd = [in_] + list(extra_reads)
        if bias is not None:
            kw["bias"] = bias
            if not isinstance(bias, float):
                rd.append(bias)
        if scale is not None:
            kw["scale"] = scale
            if not isinstance(scale, float):
                rd.append(scale)
        wr = [out]
        if accum_out is not None:
            kw["accum_out"] = accum_out
            wr.append(accum_out)
        P.add("act", lambda e: e.activation(out=out, in_=in_, func=func, **kw), reads=rd, writes=wr)

    def tt(eng, out, in0, in1, op):
        P.add(eng, lambda e: e.tensor_tensor(out=out, in0=in0, in1=in1, op=op), reads=[in0, in1], writes=[out])

    def ts(eng, out, in0, s1, op0, s2=None, op1=None):
        rd = [in0]
        if not isinstance(s1, float):
            rd.append(s1)
        if s2 is not None and not isinstance(s2, float):
            rd.append(s2)
        if op1 is None:
            P.add(eng, lambda e: e.tensor_scalar(out=out, in0=in0, scalar1=s1, scalar2=None, op0=op0), reads=rd, writes=[out])
        else:
            P.add(eng, lambda e: e.tensor_scalar(out=out, in0=in0, scalar1=s1, scalar2=s2, op0=op0, op1=op1), reads=rd, writes=[out])

    def stt(eng, out, in0, scalar, in1, op0, op1):
        rd = [in0, in1]
        if not isinstance(scalar, float):
            rd.append(scalar)
        P.add(eng, lambda e: e.scalar_tensor_tensor(out=out, in0=in0, scalar=scalar, in1=in1, op0=op0, op1=op1),
              reads=rd, writes=[out])

    def copy(eng, out, in_):
        P.add(eng, lambda e: e.tensor_copy(out=out, in_=in_), reads=[in_], writes=[out])

    def memset(eng, out, val):
        P.add(eng, lambda e: e.memset(out, val), writes=[out])

    dma_n = [0]

    def dma(eng, out, in_, sem, final=False):
        rd = [in_]
        wr = [out]
        P.add(eng, lambda e: e.dma_start(out=out, in_=in_), reads=rd, writes=wr, dsem=sem, final=final)

    wl_n = [0]

    def wload(src):
        s = wl_n[0] % 4
        wl_n[0] += 1
        dma("pool", wslot[s], src, "w%d" % s)
        return wslot[s]

    def dump(name, ap):
        if not debug:
            return
        shp = [128, int(np.prod(ap.shape[1:]))]
        d = nc.dram_tensor("dbg_" + name, list(ap.shape), ap.dtype, kind="ExternalOutput").ap()
        dbg_out[name] = d
        dma("sp", d, ap, "dbg_" + name, final=True)

    dma("sp", consts, consts_d, "consts")
    scratch_reset()
    posi = A.alloc([NPOS], I32)
    posf = A.alloc([NPOS], F32)
    bias_prev = A.alloc([2, 512], BF)
    bias_cur = A.alloc([2, 512], BF)
    bias_halo = A.alloc([2, 512], BF)
    dtmp = [A.alloc([128], F32) for _ in range(4)]
    SCR1 = A.cur

    dma("sp", posi, pos_d, "pos")
    memset("pool", ones_f, 1.0)
    memset("pool", epsT[:, 0:1], RMS_EPS)
    memset("pool", epsT[:, 1:2], LN_EPS)
    memset("pool", ones_bf, 1.0)
    memset("pool", onesm_bf, 1.0 / 512.0)
    P.add("pool", lambda e: e.affine_select(out=ident, in_=ones_bf, pattern=[[-1, 128]], compare_op=ALU.is_equal,
                                             fill=0.0, base=0, channel_multiplier=1), reads=[ones_bf], writes=[ident])
    wl_n[0] = 1
    w_kv = wload(win_d[1])
    wl_n[0] = 0
    w_q = wload(win_d[0])
    wl_n[0] = 2
    w_ga = wload(win_d[2])
    wmem_sb = A.view(gC_off, [KC, 1024], BF)
    Lw = A.view(gX_off, [16, 8, 64], BF)

    copy("dve", posf, posi)
    ts("dve", hb2, ccol(C_BGLU + 4, 4), 0.5, ALU.mult)
    act(esink, ccol(C_SINK, 4), AF.Exp)
    ts("dve", nhvB[:, 0:1], ccol(C_HV), -BIG, ALU.mult, BIG, ALU.add)

    def build_bias(dst, qcols, kcol, halo):
        d, m1, dm, m2 = dtmp
        ts("dve", d, posf[:, qcols:qcols + 128], posf[:, kcol:kcol + 1], ALU.subtract)
        ts("dve", m1, d, 0.0, ALU.is_lt, BIG, ALU.mult)
        tt("dve", dm, m1, d, ALU.add)
        ts("dve", m2, d, 128.0, ALU.is_ge, BIG, ALU.mult)
        tt("dve", dm, dm, m2, ALU.add)
        if halo:
            ts("dve", dm, dm, nhvB[:, 0:1], ALU.add)
        for g in range(2):
            for j in range(4):
                ts("dve", dst[:, g, j * 128:(j + 1) * 128], dm, -8.0 * SLOPES[4 * g + j], ALU.mult)

    build_bias(bias_prev, 128, 257, False)
    build_bias(bias_cur, 128, 258, False)
    build_bias(bias_halo, 0, 256, True)

    A.cur = SCR1
    blocks = [(0, 640), (640, 1152), (1152, 1664), (1664, 2176)]
    xs = [A.alloc([KC, 640], F32), A.alloc([KC, 512], F32)]
    sq = [A.alloc([640], BF) for _ in range(2)]
    lnr = [A.alloc([640], F32), A.alloc([512], F32)]
    rstd = [A.alloc([640], F32), A.alloc([512], F32)]
    P0_END = A.cur
    for bi, (c0, c1) in enumerate(blocks):
        w = c1 - c0
        sl = bi % 2
        for kc in range(KC):
            o_, i_ = xs[sl][:, kc, 0:w], xT_d[kc, :, c0:c1]
            extra = [w_ga] if (bi == 1 and kc == 0) else []
            P.add("sp", lambda e, o_=o_, i_=i_: e.dma_start(out=o_, in_=i_), reads=[i_] + extra, writes=[o_],
                  dsem="xs%d_%d" % (sl, kc))
        pieces = [(0, 512), (512, w)] if w > 512 else [(0, w)]
        for kc in range(KC):
            act(sq[kc % 2][:, 0:w], xs[sl][:, kc, 0:w], AF.Square)
            for pi, (a, b) in enumerate(pieces):
                mm(bank(pi, 0, b - a), ones_bf, sq[kc % 2][:, a:b], kc == 0, kc == KC - 1)
        for pi, (a, b) in enumerate(pieces):
            act(lnr[sl][:, a:b], bank(pi, 0, b - a), AF.Ln, bias=epsT[:, 0:1], scale=1.0 / D_MODEL)
        act(rstd[sl][:, 0:w], lnr[sl][:, 0:w], AF.Exp, scale=-0.5)
        for kc in range(KC):
            stt("dve", hT[:, kc, c0:c1], xs[sl][:, kc, 0:w], ccol(C_GX + kc), rstd[sl][:, 0:w], ALU.mult, ALU.mult)
    dump("hT", hT)

    def inproj_fm(ps, wsl, unit, c0, n):
        for kc in range(KC):
            mm(ps, wsl[:, kc, unit * 128:(unit + 1) * 128], hT[:, kc, c0:c0 + n], kc == 0, kc == KC - 1)

    pjn = [0]
    pj = [6, 7]

    def next_pj():
        b = pj[pjn[0] % len(pj)]
        pjn[0] += 1
        return b

    if stage >= 1:
        A.cur = SCR1
        qT = [[A.alloc([4, TG], BF) for _ in range(2)] for _ in range(2)]
        sgT = [A.alloc([4, TG], BF) for _ in range(2)]
        kT = A.alloc([2, NT], BF)
        vtok = A.alloc([17, 128], BF)
        expT = [[[A.alloc([512], BF) for _ in range(2)] for _ in range(2)] for _ in range(2)]
        lsum = A.alloc([4, 128], F32)
        rinv = A.alloc([4, 128], F32)
        tmul = A.alloc([4, 128], F32)
        A.cur = max(A.cur, P0_END)
        memT = A.alloc([KC, 256], F32)
        sqm = [A.alloc([256], BF) for _ in range(2)]
        lnm = A.alloc([256], F32)
        rm = lnm
        memn = A.alloc([KC, 256], BF)
        assert A.cur <= SCR_END, (A.cur, SCR_END)
        for kc in range(KC):
            dma("sp", memT[:, kc, :], memT_d[kc], "memT%d" % kc)
        for sl_ in range(2):
            for hf_ in range(2):
                memset("pool", qT[sl_][hf_], 0.0)
        dma("pool", wmem_sb[:, :, 0:512], wmem_d[:, :, 0:512], "wmemA")
        dma("pool", wmem_sb[:, :, 512:1024], wmem_d[:, :, 512:1024], "wmemB")
        tt("pool", E32, ident[:, 0:32], ident[:, 32:64], ALU.add)
        tt("pool", E32, E32, ident[:, 64:96], ALU.add)
        tt("pool", E32, E32, ident[:, 96:128], ALU.add)
        memset("pool", Lw, 0.0)
        for cg in range(4, 16):
            for jj in range(8):
                h_ = (cg % 2) * 32
                ts("pool", Lw[:, cg, jj, h_:h_ + 32], E32, ccol(C_WR + cg * 8 + jj), ALU.mult)

        def p1_units(tg):
            T0 = tg * TG
            c0 = HALO + T0
            sl = tg % 2
            units = []

            def u_khalo(g):
                b = next_pj()
                inproj_fm(bank(b, 0, 128), w_kv, g, 0, 128)
                copy("dve", kT[:, g, 0:128], bank(b, 0, 128))

            def u_vhalo():
                b = next_pj()
                for kc in range(KC):
                    mm(bank(b, 0, 128), hT[:, kc, 0:128], w_kv[:, kc, 256:384], kc == 0, kc == KC - 1)
                copy("dve", vtok[:, 0, :], bank(b, 0, 128))

            def u_k(g):
                b = next_pj()
                inproj_fm(bank(b), w_kv, g, c0, TG)
                copy("dve", kT[:, g, c0:c0 + TG], bank(b))

            def u_v(ti):
                b = next_pj()
                tile = 1 + tg * 4 + ti
                for kc in range(KC):
                    mm(bank(b, 0, 128), hT[:, kc, c0 + ti * 128:c0 + (ti + 1) * 128], w_kv[:, kc, 256:384], kc == 0, kc == KC - 1)
                copy("dve", vtok[:, tile, :], bank(b, 0, 128))

            def u_q(c):
                b = next_pj()
                inproj_fm(bank(b), w_q, c, c0, TG)
                copy("dve", qT[sl][0][0:64, c, :], bank(b)[0:64, :])
                copy("dve", qT[sl][1][64:128, c, :], bank(b)[64:128, :])

            def u_ga(c):
                b = next_pj()
                inproj_fm(bank(b), w_ga, c, c0, TG)
                act(sgT[sl][:, c, :], bank(b), AF.Silu)

            g0 = []
            if tg == 0:
                g0 += [lambda g=g: u_khalo(g) for g in range(2)] + [u_vhalo]
            g0 += [lambda g=g: u_k(g) for g in range(2)] + [lambda t=t: u_v(t) for t in range(4)]
            g1 = [lambda c=c: u_q(c) for c in range(4)]
            g2 = [lambda c=c: u_ga(c) for c in range(4)]
            return [g0, g1, g2]

        def p1_scores(B):
            tg, bl = B // 4, B % 4
            sl = tg % 2
            Tq = tg * TG + bl * 128
            es = B % 2
            for g in range(2):
                for kcb in range(2):
                    sb = g * 2 + kcb
                    if kcb == 1:
                        bsrc = bias_cur
                    else:
                        bsrc = bias_halo if B == 0 else bias_prev
                    mm(bank(sb), ident, bsrc[:, g, :], True, False)
                    kcol = Tq + kcb * 128
                    for j in range(4):
                        cq = 2 * g + j // 2
                        hf = j % 2
                        mm(bank(sb, j * 128, (j + 1) * 128), kT[:, g, kcol:kcol + 128],
                           qT[sl][hf][:, cq, bl * 128:(bl + 1) * 128], False, j == 3)
                    act(expT[es][g][kcb], bank(sb), AF.Exp, scale=0.125)

        def p1_pv(B):
            tg, bl = B // 4, B % 4
            sl = tg % 2
            Tq = tg * TG + bl * 128
            es = B % 2
            for bk, lhs_of in ((4, None), (5, ones_bf[:, 0:64])):
                for cq in range(4):
                    for hf in range(2):
                        h = 2 * cq + hf
                        g = h // 4
                        j = h % 4
                        for kcb in range(2):
                            lhsT = vtok[:, B + kcb, g * 64:(g + 1) * 64] if lhs_of is None else lhs_of
                            mm(bank(bk, cq * 128, (cq + 1) * 128)[hf * 64:(hf + 1) * 64, :], lhsT,
                               expT[es][g][kcb][:, j * 128:(j + 1) * 128], kcb == 0, kcb == 1)
            for cq in range(4):
                act(lsum[:, cq, :], bank(5, cq * 128, (cq + 1) * 128), AF.Ln, bias=esink[:, cq:cq + 1])
            act(rinv, lsum, AF.Exp, scale=-1.0)
            tt("dve", tmul, rinv, sgT[sl][:, :, bl * 128:(bl + 1) * 128], ALU.mult)
            tt("dve", gA[:, :, Tq:Tq + 128], bank(4).rearrange("p (a b) -> p a b", a=4), tmul, ALU.mult)

        def mem_stats():
            b = next_pj()
            for kc in range(KC):
                act(sqm[kc % 2], memT[:, kc, :], AF.Square)
                mm(bank(b, 0, 256), ones_bf, sqm[kc % 2], kc == 0, kc == KC - 1)
            act(lnm, bank(b, 0, 256), AF.Ln, bias=epsT[:, 0:1], scale=1.0 / D_MODEL)
            act(rm, lnm, AF.Exp, scale=-0.5)
            for kc in range(KC):
                stt("dve", memn[:, kc, :], memT[:, kc, :], ccol(C_GM + kc), rm, ALU.mult, ALU.mult)

        def mem_kv():
            for hd in range(4):
                b = next_pj()
                for kc in range(KC):
                    mm(bank(b, 0, 256), wmem_sb[:, kc, hd * 128:(hd + 1) * 128], memn[:, kc, :], kc == 0, kc == KC - 1)
                copy("dve", kmT[:, hd, :], bank(b, 0, 256))
            for mc in range(2):
                b = next_pj()
                for kc in range(KC):
                    mm(bank(b), memn[:, kc, mc * 128:(mc + 1) * 128], wmem_sb[:, kc, 512:1024], kc == 0, kc == KC - 1)
                copy("dve", vm[:, mc, :], bank(b))

        NB = NTG * 4
        for grp in p1_units(0):
            for u in grp:
                u()
        pending = None
        nxt = None
        for B in range(NB):
            tg, bl = B // 4, B % 4
            if bl == 0:
                nxt = p1_units(tg + 1) if tg + 1 < NTG else None
            p1_scores(B)
            if pending is not None:
                p1_pv(pending)
            pending = B
            if nxt is not None and bl < 3:
                for u in nxt[bl]:
                    u()
            if B == NB - 4:
                mem_stats()
            if B == NB - 2:
                mem_kv()
            if tg == NTG - 1 and bl == 0:
                w_ul = wload(win_d[3])
                w_ug = wload(win_d[4])
                w_gc = wload(win_d[5])
        p1_pv(pending)
        dump("gA", gA)
        dump("kT", kT)
        dump("vtok", vtok)

    if stage >= 2:
        CP = 32
        A.cur = SCR_END - (4 * (CP + NTOK) * 2 + 2 * TG * 4)
        P2_TOP = A.cur
        aT = A.alloc([4, CP + NTOK], BF)
        sig = [A.alloc([TG], F32) for _ in range(2)]
        A.cur = SCR
        A.limit = P2_TOP
        RW = 544
        Rt = A.alloc([4, 4, RW], BF)
        cfp = A.alloc([4, TG], F32)
        cbf = A.alloc([4, TG], BF)
        csq = A.alloc([4, TG], BF)
        m2 = A.alloc([TG], F32)
        var = A.alloc([TG], F32)
        lnv = var
        rsl = A.alloc([TG], F32)
        ytmp = [A.alloc([TG], F32) for _ in range(2)]
        stmp = [A.alloc([TG], BF) for _ in range(2)]
        sgc = [A.alloc([4, TG], BF) for _ in range(2)]
        A.limit = SCR_END
        memset("pool", Rt[:, :, :, RW - 8:RW], 0.0)

        pj = [4, 5, 6, 7]
        sgn = [0]

        def p2_inproj_steps(tg):
            T0 = tg * TG
            c0 = HALO + T0
            sl = tg % 2
            s0 = CP - CH + T0

            def halo():
                for c in range(4):
                    b = next_pj()
                    inproj_fm(bank(b, 0, CH), w_ug, c, HALO - CH, CH)
                    s_ = sig[sgn[0] % 2]
                    sgn[0] += 1
                    act(s_[:, 0:CH], bank(b, 0, CH), AF.Tanh, bias=hb2[:, c:c + 1], scale=0.5)
                    ts("dve", s_[:, 0:CH], s_[:, 0:CH], 1.0, ALU.add, 0.5, ALU.mult)
                    b2 = next_pj()
                    inproj_fm(bank(b2, 0, CH), w_ul, c, HALO - CH, CH)
                    stt("dve", ytmp[0][:, 0:CH], bank(b2, 0, CH), ccol(C_BGLU + c), s_[:, 0:CH], ALU.add, ALU.mult)
                    ts("dve", aT[:, c, CP - CH:CP], ytmp[0][:, 0:CH], ccol(C_HV), ALU.mult)

            def glu(c):
                b = next_pj()
                inproj_fm(bank(b), w_ug, c, c0, TG)
                s_ = sig[sgn[0] % 2]
                sgn[0] += 1
                act(s_, bank(b), AF.Tanh, bias=hb2[:, c:c + 1], scale=0.5)
                ts("dve", s_, s_, 1.0, ALU.add, 0.5, ALU.mult)
                b2 = next_pj()
                inproj_fm(bank(b2), w_ul, c, c0, TG)
                stt("dve", aT[:, c, CP + T0:CP + T0 + TG], bank(b2), ccol(C_BGLU + c), s_, ALU.add, ALU.mult)

            def relayout(cp):
                for m in range(4):
                    for r in range(4):
                        L_ = 540 if r < 3 else 539
                        dma("sp", Rt[r * 32:(r + 1) * 32, 2 * cp:2 * cp + 2, m, 0:L_],
                            aT[m * 32:(m + 1) * 32, 2 * cp:2 * cp + 2, s0 + r:s0 + r + L_], "R%d_%d_%d" % (cp, m, r))

            def gc(c):
                b = next_pj()
                inproj_fm(bank(b), w_gc, c, c0, TG)
                act(sgc[sl][:, c, :], bank(b), AF.Silu)

            st = {"glu": [lambda c=c: glu(c) for c in range(4)], "rel": [lambda cp=cp: relayout(cp) for cp in range(2)],
                  "gc": [lambda c=c: gc(c) for c in range(4)], "halo": halo}
            return st

        def p2_conv(tg):
            T0 = tg * TG
            for c in range(4):
                cb = c % 2
                for jj in range(8):
                    for m in (0, 2, 1, 3):
                        h2 = m // 2
                        mm(bank(cb)[h2 * 64:(h2 + 1) * 64, :], Lw[:, 4 * c + m, jj, :], Rt[:, c, m, 4 * jj:4 * jj + TG],
                           jj == 0 and m % 2 == 0, jj == 7 and m % 2 == 1, tile_position=(0, 64 * h2))
                ts("dve", cfp[:, c, :], bank(cb), ccol(C_BDW + c), ALU.add)
                ts("dve", cbf[:, c, :], bank(cb), ccol(C_BDW + c), ALU.add)
                act(csq[:, c, :], cfp[:, c, :], AF.Square)

        def p2_ln_steps(tg):
            T0 = tg * TG
            sl = tg % 2

            def stats():
                for c in range(4):
                    mm(bank(2), onesm_bf, cbf[:, c, :], c == 0, c == 3)
                for c in range(4):
                    mm(bank(3), onesm_bf, csq[:, c, :], c == 0, c == 3)

            def head():
                act(m2, bank(2), AF.Square)
                tt("dve", var, bank(3), m2, ALU.subtract)
                act(lnv, var, AF.Ln, bias=epsT[:, 1:2])
                act(rsl, lnv, AF.Exp, scale=-0.5)

            def chunk(c):
                y = ytmp[c % 2]
                s2 = stmp[c % 2]
                tt("dve", y, cfp[:, c, :], bank(2), ALU.subtract)
                tt("dve", y, y, rsl, ALU.mult)
                act(s2, y, AF.Silu, bias=ccol(C_LNB + c), scale=ccol(C_LNG + c))
                tt("dve", gC[:, c, T0:T0 + TG], s2, sgc[sl][:, c, :], ALU.mult)

            return {"stats": stats, "head": head, "chunk": [lambda c=c: chunk(c) for c in range(4)]}

        qcT = [A.view(SCR + i * 4096, [4, TG], BF) for i in range(2)]
        sgx = [A.view(SCR + 8192 + i * 4096, [4, TG], BF) for i in range(2)]
        p3_started = [False]

        def p3_units(tg):
            T0 = tg * TG
            c0 = HALO + T0
            sl = tg % 2

            def u_q(hd):
                b = next_pj()
                inproj_fm(bank(b), w_qc_[0], hd, c0, TG)
                copy("dve", qcT[sl][:, hd, :], bank(b))

            def u_g(hd):
                b = next_pj()
                inproj_fm(bank(b), w_gx_[0], hd, c0, TG)
                act(sgx[sl][:, hd, :], bank(b), AF.Silu)

            return [lambda h=h: u_q(h) for h in range(4)] + [lambda h=h: u_g(h) for h in range(4)]

        w_qc_ = [None]
        w_gx_ = [None]
        st0 = p2_inproj_steps(0)
        st0["halo"]()
        for f in st0["glu"][0:2]:
            f()
        st0["rel"][0]()
        for f in st0["glu"][2:4]:
            f()
        st0["rel"][1]()
        for f in st0["gc"]:
            f()
        for cg in range(4):
            for jj in range(8):
                h_ = (cg % 2) * 32
                ts("dve", Lw[:, cg, jj, h_:h_ + 32], E32, ccol(C_WR + cg * 8 + jj), ALU.mult)
        for tg in range(NTG):
            p2_conv(tg)
            ln = p2_ln_steps(tg)
            if tg + 1 < NTG:
                nx = p2_inproj_steps(tg + 1)
                slots = [[nx["glu"][0]], [nx["glu"][1], nx["rel"][0]], [], [nx["glu"][2]], [nx["glu"][3], nx["rel"][1]],
                         [nx["gc"][0]], [nx["gc"][1], nx["gc"][2], nx["gc"][3]]]
            elif stage >= 3:
                pu = p3_units(0)
                slots = [[pu[0]], [pu[1]], [], [pu[2]], [pu[3], pu[4]], [pu[5]], [pu[6], pu[7]]]
                p3_started[0] = True
            else:
                slots = [[], [], [], [], [], [], []]
            lsteps = [ln["stats"], ln["head"]] + ln["chunk"] + [lambda: None]
            for k in range(7):
                for f in slots[k]:
                    f()
                lsteps[k]()
            if tg == NTG - 2:
                w_qc_[0] = wload(win_d[6])
                w_gx_[0] = wload(win_d[7])
        dump("gC", gC)
        pass

    if stage >= 3:
        A.cur = SCR + 4 * TG * 2 * 4
        ex = [[A.alloc([TG], BF) for _ in range(2)] for _ in range(2)]
        lx = A.alloc([TG], F32)
        rx = A.alloc([TG], F32)
        tx = A.alloc([TG], F32)
        assert A.cur <= SCR + 49152
        A.cur = SCR + 49152
        wo3 = [A.alloc([4, 1024], BF) for _ in range(3)]
        pj = [6, 7]
        inv_sqrt_d = 1.0 / math.sqrt(128.0)

        def p3_scores(H):
            tg, hd = H // 4, H % 4
            sl = tg % 2
            es = hd % 2
            for mc in range(2):
                sb = (hd % 2) * 2 + mc
                mm(bank(sb), kmT[:, hd, mc * 128:(mc + 1) * 128], qcT[sl][:, hd, :], True, True)
                act(ex[es][mc], bank(sb), AF.Exp, scale=inv_sqrt_d)

        def p3_pv(H):
            tg, hd = H // 4, H % 4
            sl = tg % 2
            es = hd % 2
            T0 = tg * TG
            for mc in range(2):
                mm(bank(4), vm[:, mc, hd * 128:(hd + 1) * 128], ex[es][mc], mc == 0, mc == 1)
            for mc in range(2):
                mm(bank(5), ones_bf, ex[es][mc], mc == 0, mc == 1)
            act(lx, bank(5), AF.Ln)
            act(rx, lx, AF.Exp, scale=-1.0)
            tt("dve", tx, rx, sgx[sl][:, hd, :], ALU.mult)
            tt("dve", gX[:, hd, T0:T0 + TG], bank(4), tx, ALU.mult)

        if not p3_started[0]:
            for u in p3_units(0):
                u()
        for i in range(3):
            dma("pool", wo3[i], wo3_d[i], "wo3_%d" % i)
        pending = None
        nxt = []
        for H in range(NTG * 4):
            tg, hd = H // 4, H % 4
            if hd == 0:
                nxt = p3_units(tg + 1) if tg + 1 < NTG else []
            p3_scores(H)
            if pending is not None:
                p3_pv(pending)
            pending = H
            for u in nxt[hd * 2:hd * 2 + 2]:
                u()
            if tg == NTG - 1 and hd == 0:
                wmg = [wload(win_d[8]), wload(win_d[9])]
        p3_pv(pending)
        dump("gX", gX)

    if stage >= 4:
        A.cur = SCR
        mergedT = A.alloc([KC, NTOK], BF)
        sg = [A.alloc([TG], F32) for _ in range(4)]
        tm = [A.alloc([TG], F32) for _ in range(4)]
        assert A.cur <= SCR + 49152
        gsrc = [gA, gC, gX]
        sgi = [0]
        it = 0
        for c in range(KC):
            need = min(5, (3 * min(c + 1, KC - 1) + 2) // 4)
            while len(wmg) <= need:
                wmg.append(wload(win_d[8 + len(wmg)]))
            if c == KC - 1:
                w_o = [wload(wout_d[:, :, 0:512]), wload(wout_d[:, :, 512:1024])]
            for tg in range(NTG):
                T0 = tg * TG
                c0 = HALO + T0
                for i in range(3):
                    u = 3 * c + i
                    bg = (it * 3 + i) % 4
                    inproj_fm(bank(bg), wmg[u // 4], u % 4, c0, TG)
                    s_ = sg[sgi[0] % 4]
                    sgi[0] += 1
                    act(s_, bank(bg), AF.Sigmoid)
                    by = 4 + (it * 3 + i) % 4
                    for k4 in range(4):
                        mm(bank(by), wo3[i][:, k4, c * 128:(c + 1) * 128], gsrc[i][:, k4, T0:T0 + TG], k4 == 0, k4 == 3)
                    t_ = tm[i]
                    if i == 1:
                        stt("dve", t_, bank(by), ccol(C_BPW + c), s_, ALU.add, ALU.mult)
                    else:
                        tt("dve", t_, bank(by), s_, ALU.mult)
                tt("pool", tm[3], tm[0], tm[1], ALU.add)
                tt("dve", mergedT[:, c, T0:T0 + TG], tm[3], tm[2], ALU.add)
                it += 1
        dump("mergedT", mergedT)

    if stage >= 5:
        A.cur = SCR + 32768
        NX, NZ = 3, 4
        xsb = [A.alloc([1024], F32) for _ in range(NX)]
        zsb = [A.alloc([1024], F32) for _ in range(NZ)]
        junk = A.alloc([1024], BF)
        ssq = A.alloc([16], F32)
        lnf = A.alloc([16], F32)
        rsf = A.alloc([16], F32)
        assert A.cur <= SCR_END

        def xload(t):
            dma("sp", xsb[t % NX], xtok_d[t * 128:(t + 1) * 128, :], "xtok%d" % (t % NX))

        for t in range(min(NX, 16)):
            xload(t)
        for t in range(16):
            zb = (t % 4) * 2
            for nh in range(2):
                for kc in range(KC):
                    mm(bank(zb + nh), mergedT[:, kc, t * 128:(t + 1) * 128], w_o[nh][:, kc, :], kc == 0, kc == KC - 1)
            zps = psum[:, zb * 512:(zb + 2) * 512]
            z = zsb[t % NZ]
            tt("dve", z, zps, xsb[t % NX], ALU.add)
            if t + NX < 16:
                xload(t + NX)
            act(junk, z, AF.Square, accum_out=ssq[:, t:t + 1])
            act(lnf[:, t:t + 1], ssq[:, t:t + 1], AF.Ln, bias=epsT[:, 0:1], scale=1.0 / D_MODEL)
            act(rsf[:, t:t + 1], lnf[:, t:t + 1], AF.Exp, scale=-0.5)
            stt("dve", z, z, rsf[:, t:t + 1], ccol(C_FG, 1024), ALU.mult, ALU.mult)
            dma("pool", out_d[t * 128:(t + 1) * 128, :], z, "out%d" % (t % NZ), final=True)

    P.emit_all()
    return nc, dbg_out


def _kmajor(w, kc):
    n = w.shape[1]
    return np.ascontiguousarray(w.reshape(kc, 128, n).transpose(1, 0, 2))


def _prep_inputs(x, mem, positions, norm_g, w_in, attn_sinks, w_o_attn, b_glu, w_dw, b_dw, ln_g, ln_b,
                 w_pw, b_pw, mem_norm_g, w_mem_kv, w_o_cross, w_out, final_norm_g):
    f32 = np.float32
    x = np.asarray(x, f32)
    mem = np.asarray(mem, f32)
    positions = np.asarray(positions, np.int32)
    W = np.asarray(w_in, f32)[0]
    cols = []
    cols.append(np.arange(0, 512))
    kd = []
    for g in range(2):
        kk = 512 + g * 64 + np.arange(64)
        kd += [kk, kk]
    cols.append(np.concatenate(kd + [640 + np.arange(128), 640 + np.arange(128)]))
    cols.append(768 + np.arange(512))
    cols.append(1280 + np.arange(512))
    cols.append(1792 + np.arange(512))
    cols.append(2304 + np.arange(512))
    cols.append(2816 + np.arange(512))
    cols.append(3328 + np.arange(512))
    mg = []
    for c in range(8):
        for i in range(3):
            mg.append(3840 + i * 1024 + c * 128 + np.arange(128))
    mg = np.concatenate(mg)
    for s in range(6):
        cols.append(mg[s * 512:(s + 1) * 512])
    win = np.stack([_kmajor(W[:, cc], KC) for cc in cols], axis=0)
    wo3 = np.stack([_kmajor(np.asarray(w, f32)[0], 4) for w in (w_o_attn, w_pw, w_o_cross)], axis=0)
    wout = _kmajor(np.asarray(w_out, f32)[0], KC)
    wmem = _kmajor(np.asarray(w_mem_kv, f32)[0], KC)

    def pcol(v, n):
        return np.asarray(v, f32).reshape(n, 128).T

    cbase = np.zeros((128, NCONST), f32)
    cbase[:, C_GX:C_GX + 8] = pcol(norm_g[0], 8)
    cbase[:, C_GM:C_GM + 8] = pcol(mem_norm_g[0], 8)
    cbase[:, C_BGLU:C_BGLU + 8] = pcol(b_glu[0], 8)
    wd = np.asarray(w_dw, f32)[0, :, 0, :]
    for c in range(4):
        cbase[:, C_WDW + c * CONV_K:C_WDW + (c + 1) * CONV_K] = wd[:, c * 128:(c + 1) * 128].T
    for cg in range(16):
        for jj in range(8):
            for r in range(4):
                j = 4 * jj + r
                if j < CONV_K:
                    cbase[r * 32:(r + 1) * 32, C_WR + cg * 8 + jj] = wd[j, cg * 32:(cg + 1) * 32]
    cbase[:, C_BDW:C_BDW + 4] = pcol(b_dw[0], 4)
    cbase[:, C_LNG:C_LNG + 4] = pcol(ln_g[0], 4)
    cbase[:, C_LNB:C_LNB + 4] = pcol(ln_b[0], 4)
    cbase[:, C_BPW:C_BPW + 8] = pcol(b_pw[0], 8)
    sk = np.asarray(attn_sinks, f32)[0]
    for c in range(4):
        cbase[0:64, C_SINK + c] = sk[2 * c]
        cbase[64:128, C_SINK + c] = sk[2 * c + 1]
    cbase[:, C_FG:C_FG + 1024] = np.asarray(final_norm_g, f32)[None, :]

    in_maps = []
    for core in range(NCORES):
        b = core // 2
        st = (core % 2) * NTOK
        xo = x[b, st:st + NTOK]
        if st > 0:
            xh = x[b, st - HALO:st]
            ph = positions[b, st - HALO:st]
            hv = 1.0
        else:
            xh = np.zeros((HALO, D_MODEL), f32)
            ph = np.zeros((HALO,), np.int32)
            hv = 0.0
        xcat = np.concatenate([xh, xo], axis=0)
        xT = np.ascontiguousarray(xcat.T).reshape(KC, 128, NT)
        memT = np.ascontiguousarray(mem[b].T).reshape(KC, 128, 256)
        cst = cbase.copy()
        cst[:, C_HV] = hv
        pos = np.zeros((128, NPOS), np.int32)
        pos[:, 0:128] = positions[b, st:st + 128][None, :]
        pos[:, 128:256] = positions[b, st + 128:st + 256][None, :]
        pos[:, 256] = ph
        pos[:, 257] = positions[b, st:st + 128]
        pos[:, 258] = positions[b, st + 128:st + 256]
        in_maps.append({"xT": xT, "xtok": np.ascontiguousarray(xo), "memT": memT, "consts": cst, "pos": pos,
                        "win": win, "wo3": wo3, "wout": wout, "wmem": wmem})
    return in_maps


_CACHE = {}


def kernel(**inputs):
    in_maps = _prep_inputs(**inputs)
    if "nc" not in _CACHE:
        _CACHE["nc"] = build_program(stage=5, debug=False)[0]
    nc = _CACHE["nc"]
    res = run_bass_kernel_spmd(nc, in_maps, core_ids=list(range(NCORES)))
    out = np.empty((BATCH, SEQ, D_MODEL), np.float32)
    for core in range(NCORES):
        b = core // 2
        st = (core % 2) * NTOK
        out[b, st:st + NTOK] = res.results[core]["out"]
    return out
```

```python
import contextlib
import math
import numpy as np
import concourse.bass as bass
import concourse.mybir as mybir
from concourse.bass_utils import run_bass_kernel_spmd

DT = mybir.dt
ALU = mybir.AluOpType
AF = mybir.ActivationFunctionType
F32 = DT.float32
BF = DT.bfloat16
I32 = DT.int32
ESZ = {DT.float32: 4, DT.bfloat16: 2, DT.int32: 4}

D_MODEL = 1024
BATCH = 4
SEQ = 4096
NCORES = 8
NTOK = 2048
HALO = 128
NT = NTOK + HALO
KC = 8
CONV_K = 31
CH = CONV_K - 1
RMS_EPS = 1e-6
LN_EPS = 1e-5
SLOPES = [2.0 ** (-8.0 * (i + 1) / 8) for i in range(8)]
BIG = 1.0e30
TG = 512
NTG = NTOK // TG

C_GX = 0
C_GM = 8
C_BGLU = 16
C_WDW = 24
C_BDW = C_WDW + 124
C_LNG = C_BDW + 4
C_LNB = C_LNG + 4
C_BPW = C_LNB + 4
C_SINK = C_BPW + 8
C_HV = C_SINK + 4
C_FG = C_HV + 1
C_WR = C_FG + 1024
NCONST = C_WR + 128
NPOS = 128 + 128 + 3


def _ap_runs(ap):
    es = ESZ[ap.dtype]
    pat = [list(x) for x in ap.ap]
    pstep = pat[0][0]
    off = ap.offset % pstep if pstep > 0 else ap.offset
    free = pat[1:]
    if not free:
        return [(off * es, (off + 1) * es)]
    runs = [off]
    for step, cnt in free[:-1]:
        runs = [r + i * step for r in runs for i in range(cnt)]
    step, cnt = free[-1]
    return [(r * es, (r + (cnt - 1) * abs(step) + 1) * es) for r in runs]


class _Op:
    __slots__ = ("idx", "eng", "emit", "deps", "signal", "sigval", "sem", "inc")

    def __init__(self, idx, eng, emit):
        self.idx = idx
        self.eng = eng
        self.emit = emit
        self.deps = set()
        self.signal = False
        self.sigval = None
        self.sem = None
        self.inc = 1


class Prog:
    PAGE = 64

    def __init__(self, nc):
        self.nc = nc
        self.ops = []
        self.pages = {}
        self.final_waits = []

    def _keys(self, ap):
        t = ap.tensor
        if type(t).__name__.startswith("DRam"):
            return []
        base = t.name
        if type(t).__name__.startswith("PSum"):
            page = 2048
            quads = (0,)
        else:
            page = self.PAGE
            pat = ap.ap
            pstep, pcnt = pat[0][0], pat[0][1]
            p0 = (ap.offset // pstep) if pstep > 0 else 0
            quads = tuple(range(p0 // 32, (p0 + pcnt - 1) // 32 + 1))
        ks = []
        for lo, hi in _ap_runs(ap):
            for p in range(lo // page, (hi - 1) // page + 1):
                for q in quads:
                    ks.append((base, p, q))
        return ks

    def add(self, eng, emit, reads=(), writes=(), dsem=None, final=False):
        idx = len(self.ops)
        op = _Op(idx, eng, emit)
        if dsem is not None:
            op.sem = dsem
            op.inc = 16
            op.signal = True
        deps = set()
        pages = self.pages
        for ap in reads:
            is_psum = type(ap.tensor).__name__.startswith("PSum")
            for k in self._keys(ap):
                rec = pages.get(k)
                if rec is None:
                    rec = pages[k] = [None, []]
                if rec[0] is not None:
                    deps.add(rec[0])
                if is_psum:
                    for r in rec[1]:
                        if r != idx and self.ops[r].eng != eng:
                            deps.add(r)
                rec[1].append(idx)
        for ap in writes:
            for k in self._keys(ap):
                rec = pages.get(k)
                if rec is None:
                    rec = pages[k] = [None, []]
                if rec[0] is not None:
                    deps.add(rec[0])
                for r in rec[1]:
                    deps.add(r)
                rec[0] = idx
                rec[1] = []
        deps.discard(idx)
        best = {}
        for d in deps:
            p = self.ops[d]
            if p.sem is not None:
                op.deps.add(d)
                continue
            if p.eng == eng and eng == "pe":
                continue
            if d > best.get(p.eng, -1):
                best[p.eng] = d
        for d in best.values():
            op.deps.add(d)
            self.ops[d].signal = True
        self.ops.append(op)
        if final:
            self.final_waits.append(idx)
        return idx

    def emit_all(self):
        nc = self.nc
        counts = {}
        for op in self.ops:
            if op.signal:
                s = op.sem if op.sem is not None else op.eng
                counts[s] = counts.get(s, 0) + op.inc
                op.sigval = counts[s]
        sem_names = sorted(counts.keys())
        with contextlib.ExitStack() as st:
            sems = {n: st.enter_context(nc.semaphore("s_" + n)) for n in sem_names}
            block = st.enter_context(nc.Block())
            ops = self.ops
            final_waits = self.final_waits

            def run(engname, e):
                seen = {}
                for op in ops:
                    if op.eng != engname:
                        continue
                    need = {}
                    for d in op.deps:
                        p = ops[d]
                        s = p.sem if p.sem is not None else p.eng
                        if p.sigval > need.get(s, 0):
                            need[s] = p.sigval
                    for s, v in need.items():
                        if seen.get(s, 0) >= v:
                            continue
                        e.wait_ge(sems[s], v)
                        seen[s] = v
                    ins = op.emit(e)
                    if op.signal:
                        s = op.sem if op.sem is not None else op.eng
                        ins.then_inc(sems[s], op.inc)
                if engname == "sp":
                    for fi in final_waits:
                        p = ops[fi]
                        s = p.sem
                        if seen.get(s, 0) < p.sigval:
                            e.wait_ge(sems[s], p.sigval)
                            seen[s] = p.sigval

            @block.tensor
            def _(e):
                run("pe", e)

            @block.scalar
            def _(e):
                run("act", e)

            @block.vector
            def _(e):
                run("dve", e)

            @block.gpsimd
            def _(e):
                run("pool", e)

            @block.sync
            def _(e):
                run("sp", e)


class Arena:
    def __init__(self, nc, nbytes):
        self.t = nc.alloc_sbuf_tensor("arena", [128, nbytes // 4], F32)
        self.nbytes = nbytes
        self.cur = 0
        self.limit = nbytes

    def view(self, off, shape, dtype):
        es = ESZ[dtype]
        n = 1
        for s in shape:
            n *= s
        nb = n * es
        assert off % 4 == 0 and nb % 4 == 0, (off, nb)
        assert off + nb <= self.nbytes, (off, nb, self.nbytes)
        ap = self.t[:, off // 4:(off + nb) // 4]
        if dtype != F32:
            ap = ap.bitcast(dtype)
        if len(shape) == 2:
            ap = ap.rearrange("p (a b) -> p a b", a=shape[0])
        elif len(shape) == 3:
            ap = ap.rearrange("p (a b c) -> p a b c", a=shape[0], b=shape[1])
        return ap

    def alloc(self, shape, dtype):
        es = ESZ[dtype]
        n = 1
        for s in shape:
            n *= s
        nb = (n * es + 63) // 64 * 64
        off = self.cur
        assert off + nb <= self.limit, ("arena overflow", off, nb, self.limit)
        self.cur += nb
        return self.view(off, shape, dtype)


def build_program(stage=5, debug=False):
    nc = bass.Bass("TRN2", target_bir_lowering=False)
    dram = {}

    def din(name, shape, dtype=F32):
        dram[name] = nc.dram_tensor(name, list(shape), dtype, kind="ExternalInput").ap()
        return dram[name]

    xT_d = din("xT", [KC, 128, NT])
    xtok_d = din("xtok", [NTOK, D_MODEL])
    memT_d = din("memT", [KC, 128, 256])
    consts_d = din("consts", [128, NCONST])
    pos_d = din("pos", [128, NPOS], I32)
    win_d = din("win", [14, 128, KC, 512])
    wo3_d = din("wo3", [3, 128, 4, 1024])
    wout_d = din("wout", [128, KC, 1024])
    wmem_d = din("wmem", [128, KC, 1024])
    out_d = nc.dram_tensor("out", [NTOK, D_MODEL], F32, kind="ExternalOutput").ap()
    dbg_out = {}

    A = Arena(nc, 204800)
    psum = nc.alloc_psum_tensor("psum", [128, 4096], F32)

    def bank(b, lo=0, hi=512):
        return psum[:, b * 512 + lo:b * 512 + hi]

    P = Prog(nc)

    hT = A.alloc([KC, NT], BF)
    gA = A.alloc([4, NTOK], BF)
    gC_off = A.cur
    gC = A.alloc([4, NTOK], BF)
    gX_off = A.cur
    gX = A.alloc([4, NTOK], BF)
    wslot = [A.alloc([KC, 512], BF) for _ in range(4)]
    consts = A.alloc([NCONST], F32)
    ident = A.alloc([128], BF)
    ones_bf = A.alloc([128], BF)
    onesm_bf = A.alloc([128], BF)
    ones_f = A.alloc([128], F32)
    esink = A.alloc([4], F32)
    E32 = A.alloc([32], BF)
    hb2 = A.alloc([4], F32)
    kmT = A.alloc([4, 256], BF)
    vm = A.alloc([2, 512], BF)
    nhvB = A.alloc([2], F32)
    epsT = A.alloc([2], F32)
    SCR = A.cur
    SCR_END = A.nbytes

    def scratch_reset(base=None, limit=None):
        A.cur = SCR if base is None else base
        A.limit = SCR_END if limit is None else limit

    def ccol(c, n=1):
        return consts[:, c:c + n]

    def mm(out, lhsT, rhs, start, stop, tile_position=None):
        if tile_position is None:
            P.add("pe", lambda e: e.matmul(out, lhsT=lhsT, rhs=rhs, start=start, stop=stop),
                  reads=[lhsT, rhs], writes=[out])
        else:
            P.add("pe", lambda e: e.matmul(out, lhsT=lhsT, rhs=rhs, start=start, stop=stop, tile_position=tile_position),
                  reads=[lhsT, rhs], writes=[out])

    def act(out, in_, func, bias=None, scale=None, accum_out=None, extra_reads=()):
        kw = {}
        rd = [in_] + list(extra_reads)
        if bias is not None:
            kw["bias"] = bias
            if not isinstance(bias, float):
                rd.append(bias)
        if scale is not None:
            kw["scale"] = scale
            if not isinstance(scale, float):
                rd.append(scale)
        wr = [out]
        if accum_out is not None:
            kw["accum_out"] = accum_out
            wr.append(accum_out)
        P.add("act", lambda e: e.activation(out=out, in_=in_, func=func, **kw), reads=rd, writes=wr)

    def tt(eng, out, in0, in1, op):
        P.add(eng, lambda e: e.tensor_tensor(out=out, in0=in0, in1=in1, op=op), reads=[in0, in1], writes=[out])

    def ts(eng, out, in0, s1, op0, s2=None, op1=None):
        rd = [in0]
        if not isinstance(s1, float):
            rd.append(s1)
        if s2 is not None and not isinstance(s2, float):
            rd.append(s2)
        if op1 is None:
            P.add(eng, lambda e: e.tensor_scalar(out=out, in0=in0, scalar1=s1, scalar2=None, op0=op0), reads=rd, writes=[out])
        else:
            P.add(eng, lambda e: e.tensor_scalar(out=out, in0=in0, scalar1=s1, scalar2=s2, op0=op0, op1=op1), reads=rd, writes=[out])

    def stt(eng, out, in0, scalar, in1, op0, op1):
        rd = [in0, in1]
        if not isinstance(scalar, float):
            rd.append(scalar)
        P.add(eng, lambda e: e.scalar_tensor_tensor(out=out, in0=in0, scalar=scalar, in1=in1, op0=op0, op1=op1),
              reads=rd, writes=[out])

    def copy(eng, out, in_):
        P.add(eng, lambda e: e.tensor_copy(out=out, in_=in_), reads=[in_], writes=[out])

    def memset(eng, out, val):
        P.add(eng, lambda e: e.memset(out, val), writes=[out])

    dma_n = [0]

    def dma(eng, out, in_, sem, final=False):
        rd = [in_]
        wr = [out]
        P.add(eng, lambda e: e.dma_start(out=out, in_=in_), reads=rd, writes=wr, dsem=sem, final=final)

    wl_n = [0]

    def wload(src):
        s = wl_n[0] % 4
        wl_n[0] += 1
        dma("pool", wslot[s], src, "w%d" % s)
        return wslot[s]

    def dump(name, ap):
        if not debug:
            return
        shp = [128, int(np.prod(ap.shape[1:]))]
        d = nc.dram_tensor("dbg_" + name, list(ap.shape), ap.dtype, kind="ExternalOutput").ap()
        dbg_out[name] = d
        dma("sp", d, ap, "dbg_" + name, final=True)

    dma("sp", consts, consts_d, "consts")
    scratch_reset()
    posi = A.alloc([NPOS], I32)
    posf = A.alloc([NPOS], F32)
    bias_prev = A.alloc([2, 512], BF)
    bias_cur = A.alloc([2, 512], BF)
    bias_halo = A.alloc([2, 512], BF)
    dtmp = [A.alloc([128], F32) for _ in range(4)]
    SCR1 = A.cur

    dma("sp", posi, pos_d, "pos")
    memset("pool", ones_f, 1.0)
    memset("pool", epsT[:, 0:1], RMS_EPS)
    memset("pool", epsT[:, 1:2], LN_EPS)
    memset("pool", ones_bf, 1.0)
    memset("pool", onesm_bf, 1.0 / 512.0)
    P.add("pool", lambda e: e.affine_select(out=ident, in_=ones_bf, pattern=[[-1, 128]], compare_op=ALU.is_equal,
                                             fill=0.0, base=0, channel_multiplier=1), reads=[ones_bf], writes=[ident])
    wl_n[0] = 1
    w_kv = wload(win_d[1])
    wl_n[0] = 0
    w_q = wload(win_d[0])
    wl_n[0] = 2
    w_ga = wload(win_d[2])
    wmem_sb = A.view(gC_off, [KC, 1024], BF)
    Lw = A.view(gX_off, [16, 8, 64], BF)

    copy("dve", posf, posi)
    ts("dve", hb2, ccol(C_BGLU + 4, 4), 0.5, ALU.mult)
    act(esink, ccol(C_SINK, 4), AF.Exp)
    ts("dve", nhvB[:, 0:1], ccol(C_HV), -BIG, ALU.mult, BIG, ALU.add)

    def build_bias(dst, qcols, kcol, halo):
        d, m1, dm, m2 = dtmp
        ts("dve", d, posf[:, qcols:qcols + 128], posf[:, kcol:kcol + 1], ALU.subtract)
        ts("dve", m1, d, 0.0, ALU.is_lt, BIG, ALU.mult)
        tt("dve", dm, m1, d, ALU.add)
        ts("dve", m2, d, 128.0, ALU.is_ge, BIG, ALU.mult)
        tt("dve", dm, dm, m2, ALU.add)
        if halo:
            ts("dve", dm, dm, nhvB[:, 0:1], ALU.add)
        for g in range(2):
            for j in range(4):
                ts("dve", dst[:, g, j * 128:(j + 1) * 128], dm, -8.0 * SLOPES[4 * g + j], ALU.mult)

    build_bias(bias_prev, 128, 257, False)
    build_bias(bias_cur, 128, 258, False)
    build_bias(bias_halo, 0, 256, True)

    A.cur = SCR1
    blocks = [(0, 640), (640, 1152), (1152, 1664), (1664, 2176)]
    xs = [A.alloc([KC, 640], F32), A.alloc([KC, 512], F32)]
    sq = [A.alloc([640], BF) for _ in range(2)]
    lnr = [A.alloc([640], F32), A.alloc([512], F32)]
    rstd = [A.alloc([640], F32), A.alloc([512], F32)]
    P0_END = A.cur
    for bi, (c0, c1) in enumerate(blocks):
        w = c1 - c0
        sl = bi % 2
        for kc in range(KC):
            o_, i_ = xs[sl][:, kc, 0:w], xT_d[kc, :, c0:c1]
            extra = [w_ga] if (bi == 1 and kc == 0) else []
            P.add("sp", lambda e, o_=o_, i_=i_: e.dma_start(out=o_, in_=i_), reads=[i_] + extra, writes=[o_],
                  dsem="xs%d_%d" % (sl, kc))
        pieces = [(0, 512), (512, w)] if w > 512 else [(0, w)]
        for kc in range(KC):
            act(sq[kc % 2][:, 0:w], xs[sl][:, kc, 0:w], AF.Square)
            for pi, (a, b) in enumerate(pieces):
                mm(bank(pi, 0, b - a), ones_bf, sq[kc % 2][:, a:b], kc == 0, kc == KC - 1)
        for pi, (a, b) in enumerate(pieces):
            act(lnr[sl][:, a:b], bank(pi, 0, b - a), AF.Ln, bias=epsT[:, 0:1], scale=1.0 / D_MODEL)
        act(rstd[sl][:, 0:w], lnr[sl][:, 0:w], AF.Exp, scale=-0.5)
        for kc in range(KC):
            stt("dve", hT[:, kc, c0:c1], xs[sl][:, kc, 0:w], ccol(C_GX + kc), rstd[sl][:, 0:w], ALU.mult, ALU.mult)
    dump("hT", hT)

    def inproj_fm(ps, wsl, unit, c0, n):
        for kc in range(KC):
            mm(ps, wsl[:, kc, unit * 128:(unit + 1) * 128], hT[:, kc, c0:c0 + n], kc == 0, kc == KC - 1)

    pjn = [0]
    pj = [6, 7]

    def next_pj():
        b = pj[pjn[0] % len(pj)]
        pjn[0] += 1
        return b

    if stage >= 1:
        A.cur = SCR1
        qT = [[A.alloc([4, TG], BF) for _ in range(2)] for _ in range(2)]
        sgT = [A.alloc([4, TG], BF) for _ in range(2)]
        kT = A.alloc([2, NT], BF)
        vtok = A.alloc([17, 128], BF)
        expT = [[[A.alloc([512], BF) for _ in range(2)] for _ in range(2)] for _ in range(2)]
        lsum = A.alloc([4, 128], F32)
        rinv = A.alloc([4, 128], F32)
        tmul = A.alloc([4, 128], F32)
        A.cur = max(A.cur, P0_END)
        memT = A.alloc([KC, 256], F32)
        sqm = [A.alloc([256], BF) for _ in range(2)]
        lnm = A.alloc([256], F32)
        rm = lnm
        memn = A.alloc([KC, 256], BF)
        assert A.cur <= SCR_END, (A.cur, SCR_END)
        for kc in range(KC):
            dma("sp", memT[:, kc, :], memT_d[kc], "memT%d" % kc)
        for sl_ in range(2):
            for hf_ in range(2):
                memset("pool", qT[sl_][hf_], 0.0)
        dma("pool", wmem_sb[:, :, 0:512], wmem_d[:, :, 0:512], "wmemA")
        dma("pool", wmem_sb[:, :, 512:1024], wmem_d[:, :, 512:1024], "wmemB")
        tt("pool", E32, ident[:, 0:32], ident[:, 32:64], ALU.add)
        tt("pool", E32, E32, ident[:, 64:96], ALU.add)
        tt("pool", E32, E32, ident[:, 96:128], ALU.add)
        memset("pool", Lw, 0.0)
        for cg in range(4, 16):
            for jj in range(8):
                h_ = (cg % 2) * 32
                ts("pool", Lw[:, cg, jj, h_:h_ + 32], E32, ccol(C_WR + cg * 8 + jj), ALU.mult)

        def p1_units(tg):
            T0 = tg * TG
            c0 = HALO + T0
            sl = tg % 2
            units = []

            def u_khalo(g):
                b = next_pj()
                inproj_fm(bank(b, 0, 128), w_kv, g, 0, 128)
                copy("dve", kT[:, g, 0:128], bank(b, 0, 128))

            def u_vhalo():
                b = next_pj()
                for kc in range(KC):
                    mm(bank(b, 0, 128), hT[:, kc, 0:128], w_kv[:, kc, 256:384], kc == 0, kc == KC - 1)
                copy("dve", vtok[:, 0, :], bank(b, 0, 128))

            def u_k(g):
                b = next_pj()
                inproj_fm(bank(b), w_kv, g, c0, TG)
                copy("dve", kT[:, g, c0:c0 + TG], bank(b))

            def u_v(ti):
                b = next_pj()
                tile = 1 + tg * 4 + ti
                for kc in range(KC):
                    mm(bank(b, 0, 128), hT[:, kc, c0 + ti * 128:c0 + (ti + 1) * 128], w_kv[:, kc, 256:384], kc == 0, kc == KC - 1)
                copy("dve", vtok[:, tile, :], bank(b, 0, 128))

            def u_q(c):
                b = next_pj()
                inproj_fm(bank(b), w_q, c, c0, TG)
                copy("dve", qT[sl][0][0:64, c, :], bank(b)[0:64, :])
                copy("dve", qT[sl][1][64:128, c, :], bank(b)[64:128, :])

            def u_ga(c):
                b = next_pj()
                inproj_fm(bank(b), w_ga, c, c0, TG)
                act(sgT[sl][:, c, :], bank(b), AF.Silu)

            g0 = []
            if tg == 0:
                g0 += [lambda g=g: u_khalo(g) for g in range(2)] + [u_vhalo]
            g0 += [lambda g=g: u_k(g) for g in range(2)] + [lambda t=t: u_v(t) for t in range(4)]
            g1 = [lambda c=c: u_q(c) for c in range(4)]
            g2 = [lambda c=c: u_ga(c) for c in range(4)]
            return [g0, g1, g2]

        def p1_scores(B, groups=(0, 1)):
            tg, bl = B // 4, B % 4
            sl = tg % 2
            Tq = tg * TG + bl * 128
            es = B % 2
            for g in groups:
                for kcb in range(2):
                    sb = kcb
                    if kcb == 1:
                        bsrc = bias_cur
                    else:
                        bsrc = bias_halo if B == 0 else bias_prev
                    mm(bank(sb), ident, bsrc[:, g, :], True, False)
                    kcol = Tq + kcb * 128
                    for j in range(4):
                        cq = 2 * g + j // 2
                        hf = j % 2
                        mm(bank(sb, j * 128, (j + 1) * 128), kT[:, g, kcol:kcol + 128],
                           qT[sl][hf][:, cq, bl * 128:(bl + 1) * 128], False, j == 3)
                    act(expT[es][g][kcb], bank(sb), AF.Exp, scale=0.125)

        def p1_pv(B):
            tg, bl = B // 4, B % 4
            sl = tg % 2
            Tq = tg * TG + bl * 128
            es = B % 2
            ob, smb = (4, 5) if B % 2 == 0 else (2, 3)
            for bk, lhs_of in ((ob, None), (smb, ones_bf[:, 0:64])):
                for cq in range(4):
                    for hf in range(2):
                        h = 2 * cq + hf
                        g = h // 4
                        j = h % 4
                        for kcb in range(2):
                            lhsT = vtok[:, B + kcb, g * 64:(g + 1) * 64] if lhs_of is None else lhs_of
                            mm(bank(bk, cq * 128, (cq + 1) * 128)[hf * 64:(hf + 1) * 64, :], lhsT,
                               expT[es][g][kcb][:, j * 128:(j + 1) * 128], kcb == 0, kcb == 1)
            for cq in range(4):
                act(lsum[:, cq, :], bank(smb, cq * 128, (cq + 1) * 128), AF.Ln, bias=esink[:, cq:cq + 1])
            act(rinv, lsum, AF.Exp, scale=-1.0)
            tt("dve", tmul, rinv, sgT[sl][:, :, bl * 128:(bl + 1) * 128], ALU.mult)
            tt("dve", gA[:, :, Tq:Tq + 128], bank(ob).rearrange("p (a b) -> p a b", a=4), tmul, ALU.mult)

        def mem_stats():
            b = next_pj()
            for kc in range(KC):
                act(sqm[kc % 2], memT[:, kc, :], AF.Square)
                mm(bank(b, 0, 256), ones_bf, sqm[kc % 2], kc == 0, kc == KC - 1)
            act(lnm, bank(b, 0, 256), AF.Ln, bias=epsT[:, 0:1], scale=1.0 / D_MODEL)
            act(rm, lnm, AF.Exp, scale=-0.5)
            for kc in range(KC):
                stt("dve", memn[:, kc, :], memT[:, kc, :], ccol(C_GM + kc), rm, ALU.mult, ALU.mult)

        def mem_kv():
            for hd in range(4):
                b = next_pj()
                for kc in range(KC):
                    mm(bank(b, 0, 256), wmem_sb[:, kc, hd * 128:(hd + 1) * 128], memn[:, kc, :], kc == 0, kc == KC - 1)
                copy("dve", kmT[:, hd, :], bank(b, 0, 256))
            for mc in range(2):
                b = next_pj()
                for kc in range(KC):
                    mm(bank(b), memn[:, kc, mc * 128:(mc + 1) * 128], wmem_sb[:, kc, 512:1024], kc == 0, kc == KC - 1)
                copy("dve", vm[:, mc, :], bank(b))

        NB = NTG * 4
        for grp in p1_units(0):
            for u in grp:
                u()
        pending = None
        nxt = None
        for B in range(NB):
            tg, bl = B // 4, B % 4
            if bl == 0:
                nxt = p1_units(tg + 1) if tg + 1 < NTG else None
            p1_scores(B, (0,))
            if pending is not None:
                p1_pv(pending)
            p1_scores(B, (1,))
            pending = B
            if nxt is not None and bl < 3:
                for u in nxt[bl]:
                    u()
            if B == 1:
                mem_stats()
            if B == 3:
                mem_kv()
            if tg == NTG - 1 and bl == 0:
                w_ul = wload(win_d[3])
                w_ug = wload(win_d[4])
                w_gc = wload(win_d[5])
        p1_pv(pending)
        dump("gA", gA)
        dump("kT", kT)
        dump("vtok", vtok)

    if stage >= 2:
        CP = 32
        A.cur = SCR_END - (4 * (CP + NTOK) * 2 + 2 * TG * 4)
        P2_TOP = A.cur
        aT = A.alloc([4, CP + NTOK], BF)
        sig = [A.alloc([TG], F32) for _ in range(2)]
        A.cur = SCR
        A.limit = P2_TOP
        RW = 544
        Rt = A.alloc([4, 4, RW], BF)
        cfp = A.alloc([4, TG], F32)
        cbf = A.alloc([4, TG], BF)
        csq = A.alloc([4, TG], BF)
        m2 = A.alloc([TG], F32)
        var = A.alloc([TG], F32)
        lnv = var
        rsl = A.alloc([TG], F32)
        ytmp = [A.alloc([TG], F32) for _ in range(2)]
        stmp = [A.alloc([TG], BF) for _ in range(2)]
        sgc = [A.alloc([4, TG], BF) for _ in range(2)]
        A.limit = SCR_END
        memset("pool", Rt[:, :, :, RW - 8:RW], 0.0)

        pj = [4, 5, 6, 7]
        sgn = [0]

        def p2_inproj_steps(tg):
            T0 = tg * TG
            c0 = HALO + T0
            sl = tg % 2
            s0 = CP - CH + T0

            def halo():
                for c in range(4):
                    b = next_pj()
                    inproj_fm(bank(b, 0, CH), w_ug, c, HALO - CH, CH)
                    s_ = sig[sgn[0] % 2]
                    sgn[0] += 1
                    act(s_[:, 0:CH], bank(b, 0, CH), AF.Tanh, bias=hb2[:, c:c + 1], scale=0.5)
                    ts("dve", s_[:, 0:CH], s_[:, 0:CH], 1.0, ALU.add, 0.5, ALU.mult)
                    b2 = next_pj()
                    inproj_fm(bank(b2, 0, CH), w_ul, c, HALO - CH, CH)
                    stt("dve", ytmp[0][:, 0:CH], bank(b2, 0, CH), ccol(C_BGLU + c), s_[:, 0:CH], ALU.add, ALU.mult)
                    ts("dve", aT[:, c, CP - CH:CP], ytmp[0][:, 0:CH], ccol(C_HV), ALU.mult)

            def glu(c):
                b = next_pj()
                inproj_fm(bank(b), w_ug, c, c0, TG)
                s_ = sig[sgn[0] % 2]
                sgn[0] += 1
                act(s_, bank(b), AF.Tanh, bias=hb2[:, c:c + 1], scale=0.5)
                ts("dve", s_, s_, 1.0, ALU.add, 0.5, ALU.mult)
                b2 = next_pj()
                inproj_fm(bank(b2), w_ul, c, c0, TG)
                stt("dve", aT[:, c, CP + T0:CP + T0 + TG], bank(b2), ccol(C_BGLU + c), s_, ALU.add, ALU.mult)

            def relayout(cp):
                for m in range(4):
                    for r in range(4):
                        L_ = 540 if r < 3 else 539
                        dma("sp", Rt[r * 32:(r + 1) * 32, 2 * cp:2 * cp + 2, m, 0:L_],
                            aT[m * 32:(m + 1) * 32, 2 * cp:2 * cp + 2, s0 + r:s0 + r + L_], "R%d_%d_%d" % (cp, m, r))

            def gc(c):
                b = next_pj()
                inproj_fm(bank(b), w_gc, c, c0, TG)
                act(sgc[sl][:, c, :], bank(b), AF.Silu)

            st = {"glu": [lambda c=c: glu(c) for c in range(4)], "rel": [lambda cp=cp: relayout(cp) for cp in range(2)],
                  "gc": [lambda c=c: gc(c) for c in range(4)], "halo": halo}
            return st

        def p2_conv(tg):
            T0 = tg * TG
            for c in range(4):
                cb = c % 2
                for jj in range(8):
                    for m in (0, 2, 1, 3):
                        h2 = m // 2
                        mm(bank(cb)[h2 * 64:(h2 + 1) * 64, :], Lw[:, 4 * c + m, jj, :], Rt[:, c, m, 4 * jj:4 * jj + TG],
                           jj == 0 and m % 2 == 0, jj == 7 and m % 2 == 1, tile_position=(0, 64 * h2))
                ts("dve", cfp[:, c, :], bank(cb), ccol(C_BDW + c), ALU.add)
                ts("dve", cbf[:, c, :], bank(cb), ccol(C_BDW + c), ALU.add)
                act(csq[:, c, :], cfp[:, c, :], AF.Square)

        def p2_ln_steps(tg):
            T0 = tg * TG
            sl = tg % 2

            def stats():
                for c in range(4):
                    mm(bank(2), onesm_bf, cbf[:, c, :], c == 0, c == 3)
                for c in range(4):
                    mm(bank(3), onesm_bf, csq[:, c, :], c == 0, c == 3)

            def head():
                act(m2, bank(2), AF.Square)
                tt("dve", var, bank(3), m2, ALU.subtract)
                act(lnv, var, AF.Ln, bias=epsT[:, 1:2])
                act(rsl, lnv, AF.Exp, scale=-0.5)

            def chunk(c):
                y = ytmp[c % 2]
                s2 = stmp[c % 2]
                tt("dve", y, cfp[:, c, :], bank(2), ALU.subtract)
                tt("dve", y, y, rsl, ALU.mult)
                act(s2, y, AF.Silu, bias=ccol(C_LNB + c), scale=ccol(C_LNG + c))
                tt("dve", gC[:, c, T0:T0 + TG], s2, sgc[sl][:, c, :], ALU.mult)

            return {"stats": stats, "head": head, "chunk": [lambda c=c: chunk(c) for c in range(4)]}

        qcT = [A.view(SCR + i * 4096, [4, TG], BF) for i in range(2)]
        sgx = [A.view(SCR + 8192 + i * 4096, [4, TG], BF) for i in range(2)]
        p3_started = [False]

        def p3_units(tg):
            T0 = tg * TG
            c0 = HALO + T0
            sl = tg % 2

            def u_q(hd):
                b = next_pj()
                inproj_fm(bank(b), w_qc_[0], hd, c0, TG)
                copy("dve", qcT[sl][:, hd, :], bank(b))

            def u_g(hd):
                b = next_pj()
                inproj_fm(bank(b), w_gx_[0], hd, c0, TG)
                act(sgx[sl][:, hd, :], bank(b), AF.Silu)

            return [lambda h=h: u_q(h) for h in range(4)] + [lambda h=h: u_g(h) for h in range(4)]

        w_qc_ = [None]
        w_gx_ = [None]
        st0 = p2_inproj_steps(0)
        st0["halo"]()
        for f in st0["glu"][0:2]:
            f()
        st0["rel"][0]()
        for f in st0["glu"][2:4]:
            f()
        st0["rel"][1]()
        for f in st0["gc"]:
            f()
        for cg in range(4):
            for jj in range(8):
                h_ = (cg % 2) * 32
                ts("dve", Lw[:, cg, jj, h_:h_ + 32], E32, ccol(C_WR + cg * 8 + jj), ALU.mult)
        for tg in range(NTG):
            p2_conv(tg)
            ln = p2_ln_steps(tg)
            if tg + 1 < NTG:
                nx = p2_inproj_steps(tg + 1)
                slots = [[nx["glu"][0]], [nx["glu"][1], nx["rel"][0]], [], [nx["glu"][2]], [nx["glu"][3], nx["rel"][1]],
                         [nx["gc"][0]], [nx["gc"][1], nx["gc"][2], nx["gc"][3]]]
            elif stage >= 3:
                pu = p3_units(0)
                slots = [[pu[0]], [pu[1]], [], [pu[2]], [pu[3], pu[4]], [pu[5]], [pu[6], pu[7]]]
                p3_started[0] = True
            else:
                slots = [[], [], [], [], [], [], []]
            lsteps = [ln["stats"], ln["head"]] + ln["chunk"] + [lambda: None]
            for k in range(7):
                for f in slots[k]:
                    f()
                lsteps[k]()
            if tg == NTG - 2:
                w_qc_[0] = wload(win_d[6])
                w_gx_[0] = wload(win_d[7])
        dump("gC", gC)
        pass

    if stage >= 3:
        A.cur = SCR + 4 * TG * 2 * 4
        ex = [[A.alloc([TG], BF) for _ in range(2)] for _ in range(2)]
        lx = A.alloc([TG], F32)
        rx = A.alloc([TG], F32)
        tx = A.alloc([TG], F32)
        assert A.cur <= SCR + 49152
        A.cur = SCR + 49152
        wo3 = [A.alloc([4, 1024], BF) for _ in range(3)]
        pj = [6, 7]
        inv_sqrt_d = 1.0 / math.sqrt(128.0)

        def p3_scores(H):
            tg, hd = H // 4, H % 4
            sl = tg % 2
            es = hd % 2
            for mc in range(2):
                sb = (hd % 2) * 2 + mc
                mm(bank(sb), kmT[:, hd, mc * 128:(mc + 1) * 128], qcT[sl][:, hd, :], True, True)
                act(ex[es][mc], bank(sb), AF.Exp, scale=inv_sqrt_d)

        def p3_pv(H):
            tg, hd = H // 4, H % 4
            sl = tg % 2
            es = hd % 2
            T0 = tg * TG
            for mc in range(2):
                mm(bank(4), vm[:, mc, hd * 128:(hd + 1) * 128], ex[es][mc], mc == 0, mc == 1)
            for mc in range(2):
                mm(bank(5), ones_bf, ex[es][mc], mc == 0, mc == 1)
            act(lx, bank(5), AF.Ln)
            act(rx, lx, AF.Exp, scale=-1.0)
            tt("dve", tx, rx, sgx[sl][:, hd, :], ALU.mult)
            tt("dve", gX[:, hd, T0:T0 + TG], bank(4), tx, ALU.mult)

        if not p3_started[0]:
            for u in p3_units(0):
                u()
        for i in range(3):
            dma("pool", wo3[i], wo3_d[i], "wo3_%d" % i)
        pending = None
        nxt = []
        for H in range(NTG * 4):
            tg, hd = H // 4, H % 4
            if hd == 0:
                nxt = p3_units(tg + 1) if tg + 1 < NTG else []
            p3_scores(H)
            if pending is not None:
                p3_pv(pending)
            pending = H
            for u in nxt[hd * 2:hd * 2 + 2]:
                u()
            if tg == NTG - 1 and hd == 0:
                wmg = [wload(win_d[8]), wload(win_d[9])]
        p3_pv(pending)
        dump("gX", gX)

    if stage >= 4:
        A.cur = SCR
        mergedT = A.alloc([KC, NTOK], BF)
        sg = [A.alloc([TG], F32) for _ in range(4)]
        tm = [A.alloc([TG], F32) for _ in range(4)]
        assert A.cur <= SCR + 49152
        gsrc = [gA, gC, gX]
        sgi = [0]
        it = 0
        for c in range(KC):
            need = min(5, (3 * min(c + 1, KC - 1) + 2) // 4)
            while len(wmg) <= need:
                wmg.append(wload(win_d[8 + len(wmg)]))
            if c == KC - 1:
                w_o = [wload(wout_d[:, :, 0:512]), wload(wout_d[:, :, 512:1024])]
            for tg in range(NTG):
                T0 = tg * TG
                c0 = HALO + T0
                for i in range(3):
                    u = 3 * c + i
                    bg = (it * 3 + i) % 4
                    inproj_fm(bank(bg), wmg[u // 4], u % 4, c0, TG)
                    s_ = sg[sgi[0] % 4]
                    sgi[0] += 1
                    act(s_, bank(bg), AF.Sigmoid)
                    by = 4 + (it * 3 + i) % 4
                    for k4 in range(4):
                        mm(bank(by), wo3[i][:, k4, c * 128:(c + 1) * 128], gsrc[i][:, k4, T0:T0 + TG], k4 == 0, k4 == 3)
                    t_ = tm[i]
                    if i == 1:
                        stt("dve", t_, bank(by), ccol(C_BPW + c), s_, ALU.add, ALU.mult)
                    else:
                        tt("dve", t_, bank(by), s_, ALU.mult)
                tt("pool", tm[3], tm[0], tm[1], ALU.add)
                tt("dve", mergedT[:, c, T0:T0 + TG], tm[3], tm[2], ALU.add)
                it += 1
        dump("mergedT", mergedT)

    if stage >= 5:
        A.cur = SCR + 32768
        NX, NZ = 3, 4
        xsb = [A.alloc([1024], F32) for _ in range(NX)]
        zsb = [A.alloc([1024], F32) for _ in range(NZ)]
        junk = A.alloc([1024], BF)
        ssq = A.alloc([16], F32)
        lnf = A.alloc([16], F32)
        rsf = A.alloc([16], F32)
        assert A.cur <= SCR_END

        def xload(t):
            dma("sp", xsb[t % NX], xtok_d[t * 128:(t + 1) * 128, :], "xtok%d" % (t % NX))

        for t in range(min(NX, 16)):
            xload(t)
        for t in range(16):
            zb = (t % 4) * 2
            for nh in range(2):
                for kc in range(KC):
                    mm(bank(zb + nh), mergedT[:, kc, t * 128:(t + 1) * 128], w_o[nh][:, kc, :], kc == 0, kc == KC - 1)
            zps = psum[:, zb * 512:(zb + 2) * 512]
            z = zsb[t % NZ]
            tt("dve", z, zps, xsb[t % NX], ALU.add)
            if t + NX < 16:
                xload(t + NX)
            act(junk, z, AF.Square, accum_out=ssq[:, t:t + 1])
            act(lnf[:, t:t + 1], ssq[:, t:t + 1], AF.Ln, bias=epsT[:, 0:1], scale=1.0 / D_MODEL)
            act(rsf[:, t:t + 1], lnf[:, t:t + 1], AF.Exp, scale=-0.5)
            stt("dve", z, z, rsf[:, t:t + 1], ccol(C_FG, 1024), ALU.mult, ALU.mult)
            dma("pool", out_d[t * 128:(t + 1) * 128, :], z, "out%d" % (t % NZ), final=True)

    P.emit_all()
    return nc, dbg_out


def _kmajor(w, kc):
    n = w.shape[1]
    return np.ascontiguousarray(w.reshape(kc, 128, n).transpose(1, 0, 2))


def _prep_inputs(x, mem, positions, norm_g, w_in, attn_sinks, w_o_attn, b_glu, w_dw, b_dw, ln_g, ln_b,
                 w_pw, b_pw, mem_norm_g, w_mem_kv, w_o_cross, w_out, final_norm_g):
    f32 = np.float32
    x = np.asarray(x, f32)
    mem = np.asarray(mem, f32)
    positions = np.asarray(positions, np.int32)
    W = np.asarray(w_in, f32)[0]
    cols = []
    cols.append(np.arange(0, 512))
    kd = []
    for g in range(2):
        kk = 512 + g * 64 + np.arange(64)
        kd += [kk, kk]
    cols.append(np.concatenate(kd + [640 + np.arange(128), 640 + np.arange(128)]))
    cols.append(768 + np.arange(512))
    cols.append(1280 + np.arange(512))
    cols.append(1792 + np.arange(512))
    cols.append(2304 + np.arange(512))
    cols.append(2816 + np.arange(512))
    cols.append(3328 + np.arange(512))
    mg = []
    for c in range(8):
        for i in range(3):
            mg.append(3840 + i * 1024 + c * 128 + np.arange(128))
    mg = np.concatenate(mg)
    for s in range(6):
        cols.append(mg[s * 512:(s + 1) * 512])
    win = np.stack([_kmajor(W[:, cc], KC) for cc in cols], axis=0)
    wo3 = np.stack([_kmajor(np.asarray(w, f32)[0], 4) for w in (w_o_attn, w_pw, w_o_cross)], axis=0)
    wout = _kmajor(np.asarray(w_out, f32)[0], KC)
    wmem = _kmajor(np.asarray(w_mem_kv, f32)[0], KC)

    def pcol(v, n):
        return np.asarray(v, f32).reshape(n, 128).T

    cbase = np.zeros((128, NCONST), f32)
    cbase[:, C_GX:C_GX + 8] = pcol(norm_g[0], 8)
    cbase[:, C_GM:C_GM + 8] = pcol(mem_norm_g[0], 8)
    cbase[:, C_BGLU:C_BGLU + 8] = pcol(b_glu[0], 8)
    wd = np.asarray(w_dw, f32)[0, :, 0, :]
    for c in range(4):
        cbase[:, C_WDW + c * CONV_K:C_WDW + (c + 1) * CONV_K] = wd[:, c * 128:(c + 1) * 128].T
    for cg in range(16):
        for jj in range(8):
            for r in range(4):
                j = 4 * jj + r
                if j < CONV_K:
                    cbase[r * 32:(r + 1) * 32, C_WR + cg * 8 + jj] = wd[j, cg * 32:(cg + 1) * 32]
    cbase[:, C_BDW:C_BDW + 4] = pcol(b_dw[0], 4)
    cbase[:, C_LNG:C_LNG + 4] = pcol(ln_g[0], 4)
    cbase[:, C_LNB:C_LNB + 4] = pcol(ln_b[0], 4)
    cbase[:, C_BPW:C_BPW + 8] = pcol(b_pw[0], 8)
    sk = np.asarray(attn_sinks, f32)[0]
    for c in range(4):
        cbase[0:64, C_SINK + c] = sk[2 * c]
        cbase[64:128, C_SINK + c] = sk[2 * c + 1]
    cbase[:, C_FG:C_FG + 1024] = np.asarray(final_norm_g, f32)[None, :]

    in_maps = []
    for core in range(NCORES):
        b = core // 2
        st = (core % 2) * NTOK
        xo = x[b, st:st + NTOK]
        if st > 0:
            xh = x[b, st - HALO:st]
            ph = positions[b, st - HALO:st]
            hv = 1.0
        else:
            xh = np.zeros((HALO, D_MODEL), f32)
            ph = np.zeros((HALO,), np.int32)
            hv = 0.0
        xcat = np.concatenate([xh, xo], axis=0)
        xT = np.ascontiguousarray(xcat.T).reshape(KC, 128, NT)
        memT = np.ascontiguousarray(mem[b].T).reshape(KC, 128, 256)
        cst = cbase.copy()
        cst[:, C_HV] = hv
        pos = np.zeros((128, NPOS), np.int32)
        pos[:, 0:128] = positions[b, st:st + 128][None, :]
        pos[:, 128:256] = positions[b, st + 128:st + 256][None, :]
        pos[:, 256] = ph
        pos[:, 257] = positions[b, st:st + 128]
        pos[:, 258] = positions[b, st + 128:st + 256]
        in_maps.append({"xT": xT, "xtok": np.ascontiguousarray(xo), "memT": memT, "consts": cst, "pos": pos,
                        "win": win, "wo3": wo3, "wout": wout, "wmem": wmem})
    return in_maps


_CACHE = {}


def kernel(**inputs):
    in_maps = _prep_inputs(**inputs)
    if "nc" not in _CACHE:
        _CACHE["nc"] = build_program(stage=5, debug=False)[0]
    nc = _CACHE["nc"]
    res = run_bass_kernel_spmd(nc, in_maps, core_ids=list(range(NCORES)))
    out = np.empty((BATCH, SEQ, D_MODEL), np.float32)
    for core in range(NCORES):
        b = core // 2
        st = (core % 2) * NTOK
        out[b, st:st + NTOK] = res.results[core]["out"]
    return out
```

```python
import contextlib
import math
import numpy as np
import concourse.bass as bass
import concourse.mybir as mybir
from concourse.bass_utils import run_bass_kernel_spmd

DT = mybir.dt
ALU = mybir.AluOpType
AF = mybir.ActivationFunctionType
F32 = DT.float32
BF = DT.bfloat16
I32 = DT.int32
ESZ = {DT.float32: 4, DT.bfloat16: 2, DT.int32: 4}

D_MODEL = 1024
BATCH = 4
SEQ = 4096
NCORES = 8
NTOK = 2048
HALO = 128
NT = NTOK + HALO
KC = 8
CONV_K = 31
CH = CONV_K - 1
RMS_EPS = 1e-6
LN_EPS = 1e-5
SLOPES = [2.0 ** (-8.0 * (i + 1) / 8) for i in range(8)]
BIG = 1.0e30
TG = 512
NTG = NTOK // TG

C_GX = 0
C_GM = 8
C_BGLU = 16
C_WDW = 24
C_BDW = C_WDW + 124
C_LNG = C_BDW + 4
C_LNB = C_LNG + 4
C_BPW = C_LNB + 4
C_SINK = C_BPW + 8
C_HV = C_SINK + 4
C_FG = C_HV + 1
C_WR = C_FG + 1024
NCONST = C_WR + 128
NPOS = 128 + 128 + 3


def _ap_runs(ap):
    es = ESZ[ap.dtype]
    pat = [list(x) for x in ap.ap]
    pstep = pat[0][0]
    off = ap.offset % pstep if pstep > 0 else ap.offset
    free = pat[1:]
    if not free:
        return [(off * es, (off + 1) * es)]
    runs = [off]
    for step, cnt in free[:-1]:
        runs = [r + i * step for r in runs for i in range(cnt)]
    step, cnt = free[-1]
    return [(r * es, (r + (cnt - 1) * abs(step) + 1) * es) for r in runs]


class _Op:
    __slots__ = ("idx", "eng", "emit", "deps", "signal", "sigval", "sem", "inc")

    def __init__(self, idx, eng, emit):
        self.idx = idx
        self.eng = eng
        self.emit = emit
        self.deps = set()
        self.signal = False
        self.sigval = None
        self.sem = None
        self.inc = 1


class Prog:
    PAGE = 64

    def __init__(self, nc):
        self.nc = nc
        self.ops = []
        self.pages = {}
        self.final_waits = []

    def _keys(self, ap):
        t = ap.tensor
        if type(t).__name__.startswith("DRam"):
            return []
        base = t.name
        if type(t).__name__.startswith("PSum"):
            page = 2048
            quads = (0,)
        else:
            page = self.PAGE
            pat = ap.ap
            pstep, pcnt = pat[0][0], pat[0][1]
            p0 = (ap.offset // pstep) if pstep > 0 else 0
            quads = tuple(range(p0 // 32, (p0 + pcnt - 1) // 32 + 1))
        ks = []
        for lo, hi in _ap_runs(ap):
            for p in range(lo // page, (hi - 1) // page + 1):
                for q in quads:
                    ks.append((base, p, q))
        return ks

    def add(self, eng, emit, reads=(), writes=(), dsem=None, final=False):
        idx = len(self.ops)
        op = _Op(idx, eng, emit)
        if dsem is not None:
            op.sem = dsem
            op.inc = 16
            op.signal = True
        deps = set()
        pages = self.pages
        for ap in reads:
            is_psum = type(ap.tensor).__name__.startswith("PSum")
            for k in self._keys(ap):
                rec = pages.get(k)
                if rec is None:
                    rec = pages[k] = [None, []]
                if rec[0] is not None:
                    deps.add(rec[0])
                if is_psum:
                    for r in rec[1]:
                        if r != idx and self.ops[r].eng != eng:
                            deps.add(r)
                rec[1].append(idx)
        for ap in writes:
            for k in self._keys(ap):
                rec = pages.get(k)
                if rec is None:
                    rec = pages[k] = [None, []]
                if rec[0] is not None:
                    deps.add(rec[0])
                for r in rec[1]:
                    deps.add(r)
                rec[0] = idx
                rec[1] = []
        deps.discard(idx)
        best = {}
        for d in deps:
            p = self.ops[d]
            if p.sem is not None:
                op.deps.add(d)
                continue
            if p.eng == eng and eng == "pe":
                continue
            if d > best.get(p.eng, -1):
                best[p.eng] = d
        for d in best.values():
            op.deps.add(d)
            self.ops[d].signal = True
        self.ops.append(op)
        if final:
            self.final_waits.append(idx)
        return idx

    def emit_all(self):
        nc = self.nc
        counts = {}
        for op in self.ops:
            if op.signal:
                s = op.sem if op.sem is not None else op.eng
                counts[s] = counts.get(s, 0) + op.inc
                op.sigval = counts[s]
        sem_names = sorted(counts.keys())
        with contextlib.ExitStack() as st:
            sems = {n: st.enter_context(nc.semaphore("s_" + n)) for n in sem_names}
            block = st.enter_context(nc.Block())
            ops = self.ops
            final_waits = self.final_waits

            def run(engname, e):
                seen = {}
                for op in ops:
                    if op.eng != engname:
                        continue
                    need = {}
                    for d in op.deps:
                        p = ops[d]
                        s = p.sem if p.sem is not None else p.eng
                        if p.sigval > need.get(s, 0):
                            need[s] = p.sigval
                    for s, v in need.items():
                        if seen.get(s, 0) >= v:
                            continue
                        e.wait_ge(sems[s], v)
                        seen[s] = v
                    ins = op.emit(e)
                    if op.signal:
                        s = op.sem if op.sem is not None else op.eng
                        ins.then_inc(sems[s], op.inc)
                if engname == "sp":
                    for fi in final_waits:
                        p = ops[fi]
                        s = p.sem
                        if seen.get(s, 0) < p.sigval:
                            e.wait_ge(sems[s], p.sigval)
                            seen[s] = p.sigval

            @block.tensor
            def _(e):
                run("pe", e)

            @block.scalar
            def _(e):
                run("act", e)

            @block.vector
            def _(e):
                run("dve", e)

            @block.gpsimd
            def _(e):
                run("pool", e)

            @block.sync
            def _(e):
                run("sp", e)


class Arena:
    def __init__(self, nc, nbytes):
        self.t = nc.alloc_sbuf_tensor("arena", [128, nbytes // 4], F32)
        self.nbytes = nbytes
        self.cur = 0
        self.limit = nbytes

    def view(self, off, shape, dtype):
        es = ESZ[dtype]
        n = 1
        for s in shape:
            n *= s
        nb = n * es
        assert off % 4 == 0 and nb % 4 == 0, (off, nb)
        assert off + nb <= self.nbytes, (off, nb, self.nbytes)
        ap = self.t[:, off // 4:(off + nb) // 4]
        if dtype != F32:
            ap = ap.bitcast(dtype)
        if len(shape) == 2:
            ap = ap.rearrange("p (a b) -> p a b", a=shape[0])
        elif len(shape) == 3:
            ap = ap.rearrange("p (a b c) -> p a b c", a=shape[0], b=shape[1])
        return ap

    def alloc(self, shape, dtype):
        es = ESZ[dtype]
        n = 1
        for s in shape:
            n *= s
        nb = (n * es + 63) // 64 * 64
        off = self.cur
        assert off + nb <= self.limit, ("arena overflow", off, nb, self.limit)
        self.cur += nb
        return self.view(off, shape, dtype)


def build_program(stage=5, debug=False):
    nc = bass.Bass("TRN2", target_bir_lowering=False)
    dram = {}

    def din(name, shape, dtype=F32):
        dram[name] = nc.dram_tensor(name, list(shape), dtype, kind="ExternalInput").ap()
        return dram[name]

    xT_d = din("xT", [KC, 128, NT])
    xtok_d = din("xtok", [NTOK, D_MODEL])
    memT_d = din("memT", [KC, 128, 256])
    consts_d = din("consts", [128, NCONST])
    pos_d = din("pos", [128, NPOS], I32)
    win_d = din("win", [14, 128, KC, 512])
    wo3_d = din("wo3", [3, 128, 4, 1024])
    wout_d = din("wout", [128, KC, 1024])
    wmem_d = din("wmem", [128, KC, 1024])
    out_d = nc.dram_tensor("out", [NTOK, D_MODEL], F32, kind="ExternalOutput").ap()
    dbg_out = {}

    A = Arena(nc, 204800)
    psum = nc.alloc_psum_tensor("psum", [128, 4096], F32)

    def bank(b, lo=0, hi=512):
        return psum[:, b * 512 + lo:b * 512 + hi]

    P = Prog(nc)

    hT = A.alloc([KC, NT], BF)
    gA = A.alloc([4, NTOK], BF)
    gC_off = A.cur
    gC = A.alloc([4, NTOK], BF)
    gX_off = A.cur
    gX = A.alloc([4, NTOK], BF)
    wslot = [A.alloc([KC, 512], BF) for _ in range(4)]
    consts = A.alloc([NCONST], F32)
    ident = A.alloc([128], BF)
    ones_bf = A.alloc([128], BF)
    onesm_bf = A.alloc([128], BF)
    ones_f = A.alloc([128], F32)
    esink = A.alloc([4], F32)
    E32 = A.alloc([32], BF)
    hb2 = A.alloc([4], F32)
    kmT = A.alloc([4, 256], BF)
    vm = A.alloc([2, 512], BF)
    nhvB = A.alloc([2], F32)
    epsT = A.alloc([2], F32)
    SCR = A.cur
    SCR_END = A.nbytes

    def scratch_reset(base=None, limit=None):
        A.cur = SCR if base is None else base
        A.limit = SCR_END if limit is None else limit

    def ccol(c, n=1):
        return consts[:, c:c + n]

    def mm(out, lhsT, rhs, start, stop, tile_position=None):
        if tile_position is None:
            P.add("pe", lambda e: e.matmul(out, lhsT=lhsT, rhs=rhs, start=start, stop=stop),
                  reads=[lhsT, rhs], writes=[out])
        else:
            P.add("pe", lambda e: e.matmul(out, lhsT=lhsT, rhs=rhs, start=start, stop=stop, tile_position=tile_position),
                  reads=[lhsT, rhs], writes=[out])

    def act(out, in_, func, bias=None, scale=None, accum_out=None, extra_reads=()):
        kw = {}
        rd = [in_] + list(extra_reads)
        if bias is not None:
            kw["bias"] = bias
            if not isinstance(bias, float):
                rd.append(bias)
        if scale is not None:
            kw["scale"] = scale
            if not isinstance(scale, float):
                rd.append(scale)
        wr = [out]
        if accum_out is not None:
            kw["accum_out"] = accum_out
            wr.append(accum_out)
        P.add("act", lambda e: e.activation(out=out, in_=in_, func=func, **kw), reads=rd, writes=wr)

    def tt(eng, out, in0, in1, op):
        P.add(eng, lambda e: e.tensor_tensor(out=out, in0=in0, in1=in1, op=op), reads=[in0, in1], writes=[out])

    def ts(eng, out, in0, s1, op0, s2=None, op1=None):
        rd = [in0]
        if not isinstance(s1, float):
            rd.append(s1)
        if s2 is not None and not isinstance(s2, float):
            rd.append(s2)
        if op1 is None:
            P.add(eng, lambda e: e.tensor_scalar(out=out, in0=in0, scalar1=s1, scalar2=None, op0=op0), reads=rd, writes=[out])
        else:
            P.add(eng, lambda e: e.tensor_scalar(out=out, in0=in0, scalar1=s1, scalar2=s2, op0=op0, op1=op1), reads=rd, writes=[out])

    def stt(eng, out, in0, scalar, in1, op0, op1):
        rd = [in0, in1]
        if not isinstance(scalar, float):
            rd.append(scalar)
        P.add(eng, lambda e: e.scalar_tensor_tensor(out=out, in0=in0, scalar=scalar, in1=in1, op0=op0, op1=op1),
              reads=rd, writes=[out])

    def copy(eng, out, in_):
        P.add(eng, lambda e: e.tensor_copy(out=out, in_=in_), reads=[in_], writes=[out])

    def memset(eng, out, val):
        P.add(eng, lambda e: e.memset(out, val), writes=[out])

    dma_n = [0]

    def dma(eng, out, in_, sem, final=False):
        rd = [in_]
        wr = [out]
        P.add(eng, lambda e: e.dma_start(out=out, in_=in_), reads=rd, writes=wr, dsem=sem, final=final)

    wl_n = [0]

    def wload(src):
        s = wl_n[0] % 4
        wl_n[0] += 1
        dma("pool", wslot[s], src, "w%d" % s)
        return wslot[s]

    def dump(name, ap):
        if not debug:
            return
        shp = [128, int(np.prod(ap.shape[1:]))]
        d = nc.dram_tensor("dbg_" + name, list(ap.shape), ap.dtype, kind="ExternalOutput").ap()
        dbg_out[name] = d
        dma("sp", d, ap, "dbg_" + name, final=True)

    dma("sp", consts, consts_d, "consts")
    scratch_reset()
    posi = A.alloc([NPOS], I32)
    posf = A.alloc([NPOS], F32)
    bias_prev = A.alloc([2, 512], BF)
    bias_cur = A.alloc([2, 512], BF)
    bias_halo = A.alloc([2, 512], BF)
    dtmp = [A.alloc([128], F32) for _ in range(4)]
    SCR1 = A.cur

    dma("sp", posi, pos_d, "pos")
    memset("pool", ones_f, 1.0)
    memset("pool", epsT[:, 0:1], RMS_EPS)
    memset("pool", epsT[:, 1:2], LN_EPS)
    memset("pool", ones_bf, 1.0)
    memset("pool", onesm_bf, 1.0 / 512.0)
    P.add("pool", lambda e: e.affine_select(out=ident, in_=ones_bf, pattern=[[-1, 128]], compare_op=ALU.is_equal,
                                             fill=0.0, base=0, channel_multiplier=1), reads=[ones_bf], writes=[ident])
    wl_n[0] = 1
    w_kv = wload(win_d[1])
    wl_n[0] = 0
    w_q = wload(win_d[0])
    wl_n[0] = 2
    w_ga = wload(win_d[2])
    wmem_sb = A.view(gC_off, [KC, 1024], BF)
    Lw = A.view(gX_off, [16, 8, 64], BF)

    copy("dve", posf, posi)
    ts("dve", hb2, ccol(C_BGLU + 4, 4), 0.5, ALU.mult)
    act(esink, ccol(C_SINK, 4), AF.Exp)
    ts("dve", nhvB[:, 0:1], ccol(C_HV), -BIG, ALU.mult, BIG, ALU.add)

    def build_bias(dst, qcols, kcol, halo):
        d, m1, dm, m2 = dtmp
        ts("dve", d, posf[:, qcols:qcols + 128], posf[:, kcol:kcol + 1], ALU.subtract)
        ts("dve", m1, d, 0.0, ALU.is_lt, BIG, ALU.mult)
        tt("dve", dm, m1, d, ALU.add)
        ts("dve", m2, d, 128.0, ALU.is_ge, BIG, ALU.mult)
        tt("dve", dm, dm, m2, ALU.add)
        if halo:
            ts("dve", dm, dm, nhvB[:, 0:1], ALU.add)
        for g in range(2):
            for j in range(4):
                ts("dve", dst[:, g, j * 128:(j + 1) * 128], dm, -8.0 * SLOPES[4 * g + j], ALU.mult)

    build_bias(bias_prev, 128, 257, False)
    build_bias(bias_cur, 128, 258, False)
    build_bias(bias_halo, 0, 256, True)

    A.cur = SCR1
    blocks = [(0, 640), (640, 1152), (1152, 1664), (1664, 2176)]
    xs = [A.alloc([KC, 640], F32), A.alloc([KC, 512], F32)]
    sq = [A.alloc([640], BF) for _ in range(2)]
    lnr = [A.alloc([640], F32), A.alloc([512], F32)]
    rstd = [A.alloc([640], F32), A.alloc([512], F32)]
    P0_END = A.cur
    for bi, (c0, c1) in enumerate(blocks):
        w = c1 - c0
        sl = bi % 2
        for kc in range(KC):
            o_, i_ = xs[sl][:, kc, 0:w], xT_d[kc, :, c0:c1]
            extra = [w_ga] if (bi == 1 and kc == 0) else []
            P.add("sp", lambda e, o_=o_, i_=i_: e.dma_start(out=o_, in_=i_), reads=[i_] + extra, writes=[o_],
                  dsem="xs%d_%d" % (sl, kc))
        pieces = [(0, 512), (512, w)] if w > 512 else [(0, w)]
        for kc in range(KC):
            act(sq[kc % 2][:, 0:w], xs[sl][:, kc, 0:w], AF.Square)
            for pi, (a, b) in enumerate(pieces):
                mm(bank(pi, 0, b - a), ones_bf, sq[kc % 2][:, a:b], kc == 0, kc == KC - 1)
        for pi, (a, b) in enumerate(pieces):
            act(lnr[sl][:, a:b], bank(pi, 0, b - a), AF.Ln, bias=epsT[:, 0:1], scale=1.0 / D_MODEL)
        act(rstd[sl][:, 0:w], lnr[sl][:, 0:w], AF.Exp, scale=-0.5)
        for kc in range(KC):
            stt("dve", hT[:, kc, c0:c1], xs[sl][:, kc, 0:w], ccol(C_GX + kc), rstd[sl][:, 0:w], ALU.mult, ALU.mult)
    dump("hT", hT)

    def inproj_fm(ps, wsl, unit, c0, n):
        for kc in range(KC):
            mm(ps, wsl[:, kc, unit * 128:(unit + 1) * 128], hT[:, kc, c0:c0 + n], kc == 0, kc == KC - 1)

    pjn = [0]
    pj = [6, 7]

    def next_pj():
        b = pj[pjn[0] % len(pj)]
        pjn[0] += 1
        return b

    if stage >= 1:
        A.cur = SCR1
        qT = [[A.alloc([4, TG], BF) for _ in range(2)] for _ in range(2)]
        sgT = [A.alloc([4, TG], BF) for _ in range(2)]
        kT = A.alloc([2, NT], BF)
        vtok = A.alloc([17, 128], BF)
        expT = [[[A.alloc([512], BF) for _ in range(2)] for _ in range(2)] for _ in range(2)]
        lsum = A.alloc([4, 128], F32)
        rinv = A.alloc([4, 128], F32)
        tmul = A.alloc([4, 128], F32)
        A.cur = max(A.cur, P0_END)
        memT = A.alloc([KC, 256], F32)
        sqm = [A.alloc([256], BF) for _ in range(2)]
        lnm = A.alloc([256], F32)
        rm = lnm
        memn = A.alloc([KC, 256], BF)
        assert A.cur <= SCR_END, (A.cur, SCR_END)
        for kc in range(KC):
            dma("sp", memT[:, kc, :], memT_d[kc], "memT%d" % kc)
        for sl_ in range(2):
            for hf_ in range(2):
                memset("pool", qT[sl_][hf_], 0.0)
        dma("pool", wmem_sb[:, :, 0:512], wmem_d[:, :, 0:512], "wmemA")
        dma("pool", wmem_sb[:, :, 512:1024], wmem_d[:, :, 512:1024], "wmemB")
        tt("pool", E32, ident[:, 0:32], ident[:, 32:64], ALU.add)
        tt("pool", E32, E32, ident[:, 64:96], ALU.add)
        tt("pool", E32, E32, ident[:, 96:128], ALU.add)
        memset("pool", Lw, 0.0)
        for cg in range(4, 16):
            for jj in range(8):
                h_ = (cg % 2) * 32
                ts("pool", Lw[:, cg, jj, h_:h_ + 32], E32, ccol(C_WR + cg * 8 + jj), ALU.mult)

        def p1_units(tg):
            T0 = tg * TG
            c0 = HALO + T0
            sl = tg % 2
            units = []

            def u_khalo(g):
                b = next_pj()
                inproj_fm(bank(b, 0, 128), w_kv, g, 0, 128)
                copy("dve", kT[:, g, 0:128], bank(b, 0, 128))

            def u_vhalo():
                b = next_pj()
                for kc in range(KC):
                    mm(bank(b, 0, 128), hT[:, kc, 0:128], w_kv[:, kc, 256:384], kc == 0, kc == KC - 1)
                copy("dve", vtok[:, 0, :], bank(b, 0, 128))

            def u_k(g):
                b = next_pj()
                inproj_fm(bank(b), w_kv, g, c0, TG)
                copy("dve", kT[:, g, c0:c0 + TG], bank(b))

            def u_v(ti):
                b = next_pj()
                tile = 1 + tg * 4 + ti
                for kc in range(KC):
                    mm(bank(b, 0, 128), hT[:, kc, c0 + ti * 128:c0 + (ti + 1) * 128], w_kv[:, kc, 256:384], kc == 0, kc == KC - 1)
                copy("dve", vtok[:, tile, :], bank(b, 0, 128))

            def u_q(c):
                b = next_pj()
                inproj_fm(bank(b), w_q, c, c0, TG)
                copy("dve", qT[sl][0][0:64, c, :], bank(b)[0:64, :])
                copy("dve", qT[sl][1][64:128, c, :], bank(b)[64:128, :])

            def u_ga(c):
                b = next_pj()
                inproj_fm(bank(b), w_ga, c, c0, TG)
                act(sgT[sl][:, c, :], bank(b), AF.Silu)

            g0 = []
            if tg == 0:
                g0 += [lambda g=g: u_khalo(g) for g in range(2)] + [u_vhalo]
            g0 += [lambda g=g: u_k(g) for g in range(2)] + [lambda t=t: u_v(t) for t in range(4)]
            g1 = [lambda c=c: u_q(c) for c in range(4)]
            g2 = [lambda c=c: u_ga(c) for c in range(4)]
            return [g0, g1, g2]

        def p1_scores(B, groups=(0, 1)):
            tg, bl = B // 4, B % 4
            sl = tg % 2
            Tq = tg * TG + bl * 128
            es = B % 2
            for g in groups:
                for kcb in range(2):
                    sb = kcb
                    if kcb == 1:
                        bsrc = bias_cur
                    else:
                        bsrc = bias_halo if B == 0 else bias_prev
                    mm(bank(sb), ident, bsrc[:, g, :], True, False)
                    kcol = Tq + kcb * 128
                    for j in range(4):
                        cq = 2 * g + j // 2
                        hf = j % 2
                        mm(bank(sb, j * 128, (j + 1) * 128), kT[:, g, kcol:kcol + 128],
                           qT[sl][hf][:, cq, bl * 128:(bl + 1) * 128], False, j == 3)
                    act(expT[es][g][kcb], bank(sb), AF.Exp, scale=0.125)

        def p1_pv(B):
            tg, bl = B // 4, B % 4
            sl = tg % 2
            Tq = tg * TG + bl * 128
            es = B % 2
            ob, smb = (4, 5) if B % 2 == 0 else (2, 3)
            for bk, lhs_of in ((ob, None), (smb, ones_bf[:, 0:64])):
                for cq in range(4):
                    for hf in range(2):
                        h = 2 * cq + hf
                        g = h // 4
                        j = h % 4
                        for kcb in range(2):
                            lhsT = vtok[:, B + kcb, g * 64:(g + 1) * 64] if lhs_of is None else lhs_of
                            mm(bank(bk, cq * 128, (cq + 1) * 128)[hf * 64:(hf + 1) * 64, :], lhsT,
                               expT[es][g][kcb][:, j * 128:(j + 1) * 128], kcb == 0, kcb == 1)
            for cq in range(4):
                act(lsum[:, cq, :], bank(smb, cq * 128, (cq + 1) * 128), AF.Ln, bias=esink[:, cq:cq + 1])
            act(rinv, lsum, AF.Exp, scale=-1.0)
            tt("dve", tmul, rinv, sgT[sl][:, :, bl * 128:(bl + 1) * 128], ALU.mult)
            tt("dve", gA[:, :, Tq:Tq + 128], bank(ob).rearrange("p (a b) -> p a b", a=4), tmul, ALU.mult)

        def mem_stats():
            b = next_pj()
            for kc in range(KC):
                act(sqm[kc % 2], memT[:, kc, :], AF.Square)
                mm(bank(b, 0, 256), ones_bf, sqm[kc % 2], kc == 0, kc == KC - 1)
            act(lnm, bank(b, 0, 256), AF.Ln, bias=epsT[:, 0:1], scale=1.0 / D_MODEL)
            act(rm, lnm, AF.Exp, scale=-0.5)
            for kc in range(KC):
                stt("dve", memn[:, kc, :], memT[:, kc, :], ccol(C_GM + kc), rm, ALU.mult, ALU.mult)

        def mem_kv():
            for hd in range(4):
                b = next_pj()
                for kc in range(KC):
                    mm(bank(b, 0, 256), wmem_sb[:, kc, hd * 128:(hd + 1) * 128], memn[:, kc, :], kc == 0, kc == KC - 1)
                copy("dve", kmT[:, hd, :], bank(b, 0, 256))
            for mc in range(2):
                b = next_pj()
                for kc in range(KC):
                    mm(bank(b), memn[:, kc, mc * 128:(mc + 1) * 128], wmem_sb[:, kc, 512:1024], kc == 0, kc == KC - 1)
                copy("dve", vm[:, mc, :], bank(b))

        NB = NTG * 4
        for grp in p1_units(0):
            for u in grp:
                u()
        pending = None
        nxt = None
        for B in range(NB):
            tg, bl = B // 4, B % 4
            if bl == 0:
                nxt = p1_units(tg + 1) if tg + 1 < NTG else None
            p1_scores(B, (0,))
            if pending is not None:
                p1_pv(pending)
            p1_scores(B, (1,))
            pending = B
            if nxt is not None and bl < 3:
                for u in nxt[bl]:
                    u()
            if B == 1:
                mem_stats()
            if B == 3:
                mem_kv()
            if tg == NTG - 1 and bl == 0:
                w_ul = wload(win_d[3])
                w_ug = wload(win_d[4])
                w_gc = wload(win_d[5])
        p1_pv(pending)
        dump("gA", gA)
        dump("kT", kT)
        dump("vtok", vtok)

    if stage >= 2:
        CP = 32
        A.cur = SCR_END - (4 * (CP + NTOK) * 2 + 2 * TG * 4)
        P2_TOP = A.cur
        aT = A.alloc([4, CP + NTOK], BF)
        sig = [A.alloc([TG], F32) for _ in range(2)]
        A.cur = SCR
        A.limit = P2_TOP
        RW = 544
        Rt = A.alloc([4, 4, RW], BF)
        cfp = A.alloc([4, TG], F32)
        cbf = A.alloc([4, TG], BF)
        csq = A.alloc([4, TG], BF)
        m2 = A.alloc([TG], F32)
        var = A.alloc([TG], F32)
        lnv = var
        rsl = A.alloc([TG], F32)
        ytmp = [A.alloc([TG], F32) for _ in range(2)]
        stmp = [A.alloc([TG], BF) for _ in range(2)]
        sgc = [A.alloc([4, TG], BF) for _ in range(2)]
        A.limit = SCR_END
        memset("pool", Rt[:, :, :, RW - 8:RW], 0.0)

        pj = [4, 5, 6, 7]
        sgn = [0]

        def p2_inproj_steps(tg):
            T0 = tg * TG
            c0 = HALO + T0
            sl = tg % 2
            s0 = CP - CH + T0

            def halo():
                for c in range(4):
                    b = next_pj()
                    inproj_fm(bank(b, 0, CH), w_ug, c, HALO - CH, CH)
                    s_ = sig[sgn[0] % 2]
                    sgn[0] += 1
                    act(s_[:, 0:CH], bank(b, 0, CH), AF.Tanh, bias=hb2[:, c:c + 1], scale=0.5)
                    ts("dve", s_[:, 0:CH], s_[:, 0:CH], 1.0, ALU.add, 0.5, ALU.mult)
                    b2 = next_pj()
                    inproj_fm(bank(b2, 0, CH), w_ul, c, HALO - CH, CH)
                    stt("dve", ytmp[0][:, 0:CH], bank(b2, 0, CH), ccol(C_BGLU + c), s_[:, 0:CH], ALU.add, ALU.mult)
                    ts("dve", aT[:, c, CP - CH:CP], ytmp[0][:, 0:CH], ccol(C_HV), ALU.mult)

            def glu(c):
                b = next_pj()
                inproj_fm(bank(b), w_ug, c, c0, TG)
                s_ = sig[sgn[0] % 2]
                sgn[0] += 1
                act(s_, bank(b), AF.Tanh, bias=hb2[:, c:c + 1], scale=0.5)
                ts("dve", s_, s_, 1.0, ALU.add, 0.5, ALU.mult)
                b2 = next_pj()
                inproj_fm(bank(b2), w_ul, c, c0, TG)
                stt("dve", aT[:, c, CP + T0:CP + T0 + TG], bank(b2), ccol(C_BGLU + c), s_, ALU.add, ALU.mult)

            def relayout(cp):
                for m in range(4):
                    for r in range(4):
                        L_ = 540 if r < 3 else 539
                        dma("sp", Rt[r * 32:(r + 1) * 32, 2 * cp:2 * cp + 2, m, 0:L_],
                            aT[m * 32:(m + 1) * 32, 2 * cp:2 * cp + 2, s0 + r:s0 + r + L_], "R%d_%d_%d" % (cp, m, r))

            def gc(c):
                b = next_pj()
                inproj_fm(bank(b), w_gc, c, c0, TG)
                act(sgc[sl][:, c, :], bank(b), AF.Silu)

            st = {"glu": [lambda c=c: glu(c) for c in range(4)], "rel": [lambda cp=cp: relayout(cp) for cp in range(2)],
                  "gc": [lambda c=c: gc(c) for c in range(4)], "halo": halo}
            return st

        def p2_conv(tg):
            T0 = tg * TG
            for c in range(4):
                cb = c % 2
                for jj in range(8):
                    for m in (0, 2, 1, 3):
                        h2 = m // 2
                        mm(bank(cb)[h2 * 64:(h2 + 1) * 64, :], Lw[:, 4 * c + m, jj, :], Rt[:, c, m, 4 * jj:4 * jj + TG],
                           jj == 0 and m % 2 == 0, jj == 7 and m % 2 == 1, tile_position=(0, 64 * h2))
                ts("dve", cfp[:, c, :], bank(cb), ccol(C_BDW + c), ALU.add)
                ts("dve", cbf[:, c, :], bank(cb), ccol(C_BDW + c), ALU.add)
                act(csq[:, c, :], cfp[:, c, :], AF.Square)

        def p2_ln_steps(tg):
            T0 = tg * TG
            sl = tg % 2

            def stats():
                for c in range(4):
                    mm(bank(2), onesm_bf, cbf[:, c, :], c == 0, c == 3)
                for c in range(4):
                    mm(bank(3), onesm_bf, csq[:, c, :], c == 0, c == 3)

            def head():
                act(m2, bank(2), AF.Square)
                tt("dve", var, bank(3), m2, ALU.subtract)
                act(lnv, var, AF.Ln, bias=epsT[:, 1:2])
                act(rsl, lnv, AF.Exp, scale=-0.5)

            def chunk(c):
                y = ytmp[c % 2]
                s2 = stmp[c % 2]
                tt("dve", y, cfp[:, c, :], bank(2), ALU.subtract)
                tt("dve", y, y, rsl, ALU.mult)
                act(s2, y, AF.Silu, bias=ccol(C_LNB + c), scale=ccol(C_LNG + c))
                tt("dve", gC[:, c, T0:T0 + TG], s2, sgc[sl][:, c, :], ALU.mult)

            return {"stats": stats, "head": head, "chunk": [lambda c=c: chunk(c) for c in range(4)]}

        qcT = [A.view(SCR + i * 4096, [4, TG], BF) for i in range(2)]
        sgx = [A.view(SCR + 8192 + i * 4096, [4, TG], BF) for i in range(2)]
        p3_started = [False]

        def p3_units(tg):
            T0 = tg * TG
            c0 = HALO + T0
            sl = tg % 2

            def u_q(hd):
                b = next_pj()
                inproj_fm(bank(b), w_qc_[0], hd, c0, TG)
                copy("dve", qcT[sl][:, hd, :], bank(b))

            def u_g(hd):
                b = next_pj()
                inproj_fm(bank(b), w_gx_[0], hd, c0, TG)
                act(sgx[sl][:, hd, :], bank(b), AF.Silu)

            return [lambda h=h: u_q(h) for h in range(4)] + [lambda h=h: u_g(h) for h in range(4)]

        w_qc_ = [None]
        w_gx_ = [None]
        st0 = p2_inproj_steps(0)
        st0["halo"]()
        for f in st0["glu"][0:2]:
            f()
        st0["rel"][0]()
        for f in st0["glu"][2:4]:
            f()
        st0["rel"][1]()
        for f in st0["gc"]:
            f()
        for cg in range(4):
            for jj in range(8):
                h_ = (cg % 2) * 32
                ts("dve", Lw[:, cg, jj, h_:h_ + 32], E32, ccol(C_WR + cg * 8 + jj), ALU.mult)
        for tg in range(NTG):
            p2_conv(tg)
            ln = p2_ln_steps(tg)
            if tg + 1 < NTG:
                nx = p2_inproj_steps(tg + 1)
                slots = [[nx["glu"][0]], [nx["glu"][1], nx["rel"][0]], [], [nx["glu"][2]], [nx["glu"][3], nx["rel"][1]],
                         [nx["gc"][0]], [nx["gc"][1], nx["gc"][2], nx["gc"][3]]]
            elif stage >= 3:
                pu = p3_units(0)
                slots = [[pu[0]], [pu[1]], [], [pu[2]], [pu[3], pu[4]], [pu[5]], [pu[6], pu[7]]]
                p3_started[0] = True
            else:
                slots = [[], [], [], [], [], [], []]
            lsteps = [ln["stats"], ln["head"]] + ln["chunk"] + [lambda: None]
            for k in range(7):
                for f in slots[k]:
                    f()
                lsteps[k]()
            if tg == NTG - 2:
                w_qc_[0] = wload(win_d[6])
                w_gx_[0] = wload(win_d[7])
        dump("gC", gC)
        pass

    if stage >= 3:
        A.cur = SCR + 4 * TG * 2 * 4
        ex = [[A.alloc([TG], BF) for _ in range(2)] for _ in range(2)]
        lx = A.alloc([TG], F32)
        rx = A.alloc([TG], F32)
        tx = A.alloc([TG], F32)
        assert A.cur <= SCR + 49152
        A.cur = SCR + 49152
        wo3 = [A.alloc([4, 1024], BF) for _ in range(3)]
        pj = [6, 7]
        inv_sqrt_d = 1.0 / math.sqrt(128.0)

        def p3_scores(H):
            tg, hd = H // 4, H % 4
            sl = tg % 2
            es = hd % 2
            for mc in range(2):
                sb = mc
                mm(bank(sb), kmT[:, hd, mc * 128:(mc + 1) * 128], qcT[sl][:, hd, :], True, True)
                act(ex[es][mc], bank(sb), AF.Exp, scale=inv_sqrt_d)

        def p3_pv(H):
            tg, hd = H // 4, H % 4
            sl = tg % 2
            es = hd % 2
            T0 = tg * TG
            ob, smb = (4, 5) if H % 2 == 0 else (2, 3)
            for mc in range(2):
                mm(bank(ob), vm[:, mc, hd * 128:(hd + 1) * 128], ex[es][mc], mc == 0, mc == 1)
            for mc in range(2):
                mm(bank(smb), ones_bf, ex[es][mc], mc == 0, mc == 1)
            act(lx, bank(smb), AF.Ln)
            act(rx, lx, AF.Exp, scale=-1.0)
            tt("dve", tx, rx, sgx[sl][:, hd, :], ALU.mult)
            tt("dve", gX[:, hd, T0:T0 + TG], bank(ob), tx, ALU.mult)

        if not p3_started[0]:
            for u in p3_units(0):
                u()
        for i in range(3):
            dma("pool", wo3[i], wo3_d[i], "wo3_%d" % i)
        pending = None
        nxt = []
        for H in range(NTG * 4):
            tg, hd = H // 4, H % 4
            if hd == 0:
                nxt = p3_units(tg + 1) if tg + 1 < NTG else []
            p3_scores(H)
            if pending is not None:
                p3_pv(pending)
            pending = H
            for u in nxt[hd * 2:hd * 2 + 2]:
                u()
            if tg == NTG - 1 and hd == 0:
                wmg = [wload(win_d[8]), wload(win_d[9])]
        p3_pv(pending)
        dump("gX", gX)

    if stage >= 4:
        A.cur = SCR
        mergedT = A.alloc([KC, NTOK], BF)
        sg = [A.alloc([TG], F32) for _ in range(4)]
        tm = [A.alloc([TG], F32) for _ in range(4)]
        assert A.cur <= SCR + 49152
        gsrc = [gA, gC, gX]
        sgi = [0]
        it = 0
        for c in range(KC):
            need = min(5, (3 * min(c + 1, KC - 1) + 2) // 4)
            while len(wmg) <= need:
                wmg.append(wload(win_d[8 + len(wmg)]))
            if c == KC - 1:
                w_o = [wload(wout_d[:, :, 0:512]), wload(wout_d[:, :, 512:1024])]
            for tg in range(NTG):
                T0 = tg * TG
                c0 = HALO + T0
                for i in range(3):
                    u = 3 * c + i
                    bg = (it * 3 + i) % 4
                    inproj_fm(bank(bg), wmg[u // 4], u % 4, c0, TG)
                    s_ = sg[sgi[0] % 4]
                    sgi[0] += 1
                    act(s_, bank(bg), AF.Sigmoid)
                    by = 4 + (it * 3 + i) % 4
                    for k4 in range(4):
                        mm(bank(by), wo3[i][:, k4, c * 128:(c + 1) * 128], gsrc[i][:, k4, T0:T0 + TG], k4 == 0, k4 == 3)
                    t_ = tm[i]
                    if i == 1:
                        stt("dve", t_, bank(by), ccol(C_BPW + c), s_, ALU.add, ALU.mult)
                    else:
                        tt("dve", t_, bank(by), s_, ALU.mult)
                tt("pool", tm[3], tm[0], tm[1], ALU.add)
                tt("dve", mergedT[:, c, T0:T0 + TG], tm[3], tm[2], ALU.add)
                it += 1
        dump("mergedT", mergedT)

    if stage >= 5:
        A.cur = SCR + 32768
        NX, NZ = 3, 4
        xsb = [A.alloc([1024], F32) for _ in range(NX)]
        zsb = [A.alloc([1024], F32) for _ in range(NZ)]
        junk = A.alloc([1024], BF)
        ssq = A.alloc([16], F32)
        lnf = A.alloc([16], F32)
        rsf = A.alloc([16], F32)
        assert A.cur <= SCR_END

        def xload(t):
            dma("sp", xsb[t % NX], xtok_d[t * 128:(t + 1) * 128, :], "xtok%d" % (t % NX))

        for t in range(min(NX, 16)):
            xload(t)
        for t in range(16):
            zb = (t % 4) * 2
            for nh in range(2):
                for kc in range(KC):
                    mm(bank(zb + nh), mergedT[:, kc, t * 128:(t + 1) * 128], w_o[nh][:, kc, :], kc == 0, kc == KC - 1)
            zps = psum[:, zb * 512:(zb + 2) * 512]
            z = zsb[t % NZ]
            tt("dve", z, zps, xsb[t % NX], ALU.add)
            if t + NX < 16:
                xload(t + NX)
            act(junk, z, AF.Square, accum_out=ssq[:, t:t + 1])
            act(lnf[:, t:t + 1], ssq[:, t:t + 1], AF.Ln, bias=epsT[:, 0:1], scale=1.0 / D_MODEL)
            act(rsf[:, t:t + 1], lnf[:, t:t + 1], AF.Exp, scale=-0.5)
            stt("dve", z, z, rsf[:, t:t + 1], ccol(C_FG, 1024), ALU.mult, ALU.mult)
            dma("pool", out_d[t * 128:(t + 1) * 128, :], z, "out%d" % (t % NZ), final=True)

    P.emit_all()
    return nc, dbg_out


def _kmajor(w, kc):
    n = w.shape[1]
    return np.ascontiguousarray(w.reshape(kc, 128, n).transpose(1, 0, 2))


def _prep_inputs(x, mem, positions, norm_g, w_in, attn_sinks, w_o_attn, b_glu, w_dw, b_dw, ln_g, ln_b,
                 w_pw, b_pw, mem_norm_g, w_mem_kv, w_o_cross, w_out, final_norm_g):
    f32 = np.float32
    x = np.asarray(x, f32)
    mem = np.asarray(mem, f32)
    positions = np.asarray(positions, np.int32)
    W = np.asarray(w_in, f32)[0]
    cols = []
    cols.append(np.arange(0, 512))
    kd = []
    for g in range(2):
        kk = 512 + g * 64 + np.arange(64)
        kd += [kk, kk]
    cols.append(np.concatenate(kd + [640 + np.arange(128), 640 + np.arange(128)]))
    cols.append(768 + np.arange(512))
    cols.append(1280 + np.arange(512))
    cols.append(1792 + np.arange(512))
    cols.append(2304 + np.arange(512))
    cols.append(2816 + np.arange(512))
    cols.append(3328 + np.arange(512))
    mg = []
    for c in range(8):
        for i in range(3):
            mg.append(3840 + i * 1024 + c * 128 + np.arange(128))
    mg = np.concatenate(mg)
    for s in range(6):
        cols.append(mg[s * 512:(s + 1) * 512])
    win = np.stack([_kmajor(W[:, cc], KC) for cc in cols], axis=0)
    wo3 = np.stack([_kmajor(np.asarray(w, f32)[0], 4) for w in (w_o_attn, w_pw, w_o_cross)], axis=0)
    wout = _kmajor(np.asarray(w_out, f32)[0], KC)
    wmem = _kmajor(np.asarray(w_mem_kv, f32)[0], KC)

    def pcol(v, n):
        return np.asarray(v, f32).reshape(n, 128).T

    cbase = np.zeros((128, NCONST), f32)
    cbase[:, C_GX:C_GX + 8] = pcol(norm_g[0], 8)
    cbase[:, C_GM:C_GM + 8] = pcol(mem_norm_g[0], 8)
    cbase[:, C_BGLU:C_BGLU + 8] = pcol(b_glu[0], 8)
    wd = np.asarray(w_dw, f32)[0, :, 0, :]
    for c in range(4):
        cbase[:, C_WDW + c * CONV_K:C_WDW + (c + 1) * CONV_K] = wd[:, c * 128:(c + 1) * 128].T
    for cg in range(16):
        for jj in range(8):
            for r in range(4):
                j = 4 * jj + r
                if j < CONV_K:
                    cbase[r * 32:(r + 1) * 32, C_WR + cg * 8 + jj] = wd[j, cg * 32:(cg + 1) * 32]
    cbase[:, C_BDW:C_BDW + 4] = pcol(b_dw[0], 4)
    cbase[:, C_LNG:C_LNG + 4] = pcol(ln_g[0], 4)
    cbase[:, C_LNB:C_LNB + 4] = pcol(ln_b[0], 4)
    cbase[:, C_BPW:C_BPW + 8] = pcol(b_pw[0], 8)
    sk = np.asarray(attn_sinks, f32)[0]
    for c in range(4):
        cbase[0:64, C_SINK + c] = sk[2 * c]
        cbase[64:128, C_SINK + c] = sk[2 * c + 1]
    cbase[:, C_FG:C_FG + 1024] = np.asarray(final_norm_g, f32)[None, :]

    in_maps = []
    for core in range(NCORES):
        b = core // 2
        st = (core % 2) * NTOK
        xo = x[b, st:st + NTOK]
        if st > 0:
            xh = x[b, st - HALO:st]
            ph = positions[b, st - HALO:st]
            hv = 1.0
        else:
            xh = np.zeros((HALO, D_MODEL), f32)
            ph = np.zeros((HALO,), np.int32)
            hv = 0.0
        xcat = np.concatenate([xh, xo], axis=0)
        xT = np.ascontiguousarray(xcat.T).reshape(KC, 128, NT)
        memT = np.ascontiguousarray(mem[b].T).reshape(KC, 128, 256)
        cst = cbase.copy()
        cst[:, C_HV] = hv
        pos = np.zeros((128, NPOS), np.int32)
        pos[:, 0:128] = positions[b, st:st + 128][None, :]
        pos[:, 128:256] = positions[b, st + 128:st + 256][None, :]
        pos[:, 256] = ph
        pos[:, 257] = positions[b, st:st + 128]
        pos[:, 258] = positions[b, st + 128:st + 256]
        in_maps.append({"xT": xT, "xtok": np.ascontiguousarray(xo), "memT": memT, "consts": cst, "pos": pos,
                        "win": win, "wo3": wo3, "wout": wout, "wmem": wmem})
    return in_maps


_CACHE = {}


def kernel(**inputs):
    in_maps = _prep_inputs(**inputs)
    if "nc" not in _CACHE:
        _CACHE["nc"] = build_program(stage=5, debug=False)[0]
    nc = _CACHE["nc"]
    res = run_bass_kernel_spmd(nc, in_maps, core_ids=list(range(NCORES)))
    out = np.empty((BATCH, SEQ, D_MODEL), np.float32)
    for core in range(NCORES):
        b = core // 2
        st = (core % 2) * NTOK
        out[b, st:st + NTOK] = res.results[core]["out"]
    return out
```
